# Optimizing a Trainium2 kernel written in Bass

```python
import jax
import jax.numpy as jnp
from jax import lax
import numpy as np

D_MODEL = 2048
BATCH = 8
SEQ = 2048
DEPTH = 2
DEC_BATCH = 128
DEC_SEQ = 1
PAST_LEN = 2048
PAGE_SIZE = 128

N_MIX_LAYERS = (DEPTH + 1) // 2
N_ATTN_LAYERS = DEPTH // 2
A_WIDTH = D_MODEL // 2
A_GROUP = 128
A_GROUPS = A_WIDTH // A_GROUP
CHUNK = 128
B_WIDTH = D_MODEL - A_WIDTH
CONV_W = 3
MIX_SPLITS = (A_WIDTH, 2 * A_WIDTH, 2 * A_WIDTH + B_WIDTH, 2 * A_WIDTH + 2 * B_WIDTH)
MIX_IN = 2 * A_WIDTH + 3 * B_WIDTH
HEAD_DIM = 128
N_HEADS = D_MODEL // HEAD_DIM
N_KV_HEADS = 4
GQA_GROUP = N_HEADS // N_KV_HEADS
IDX_HEADS = 16
IDX_DIM = 64
TOPK_MAX = 256
QBLOCK = 128
Q_W = N_HEADS * HEAD_DIM
KV_W = N_KV_HEADS * HEAD_DIM
QI_W = IDX_HEADS * IDX_DIM
ATTN_SPLITS = (Q_W, Q_W + KV_W, Q_W + 2 * KV_W, Q_W + 2 * KV_W + QI_W, Q_W + 2 * KV_W + QI_W + IDX_DIM)
ATTN_IN = Q_W + 2 * KV_W + QI_W + IDX_DIM + IDX_HEADS
ATTN_SCALE = HEAD_DIM ** -0.5
IDX_SCALE = (IDX_DIM * IDX_HEADS) ** -0.5
MEM_LEN = 256
MEM_HEADS = 4
MEM_HEAD_DIM = 128
MEM_W = MEM_HEADS * MEM_HEAD_DIM
MEM_SCALE = MEM_HEAD_DIM ** -0.5
D_FF = 5632
LN_EPS = 1e-5
ALPHA = (2 * DEPTH) ** 0.25
BETA = (8 * DEPTH) ** -0.25

kernel_name = 'hybrid_sgu_conv_dsa_decoder_step'


def layer_norm(x, g, b):
    xf = x.astype(jnp.float32)
    xc = xf - jnp.mean(xf, axis=-1, keepdims=True)
    var = jnp.mean(xc * xc, axis=-1, keepdims=True)
    y = xc * lax.rsqrt(var + LN_EPS) * g.astype(jnp.float32) + b.astype(jnp.float32)
    return y.astype(x.dtype)


def post_ln(h, f, g, b):
    return layer_norm(ALPHA * h + f, g, b)


def swiglu(x, w_gu, w_down):
    gate, up = jnp.split(x @ w_gu, 2, axis=-1)
    return (jax.nn.silu(gate) * up) @ w_down


def gather_rows(rows, idx):
    return jax.vmap(lambda r, i: r[i])(rows, idx)


def sgu(u, vn, w_s, b_s):
    b, t = u.shape[:2]
    n = min(t, CHUNK)
    c = t // n
    mask = jnp.tril(jnp.ones((n, n), dtype=bool))
    w = jnp.where(mask[None], w_s[:, :n, :n], 0)
    vc = vn.reshape(b, c, n, A_GROUPS, A_GROUP)
    mixed = jnp.einsum('gts,bcsgd->bctgd', w, vc) + jnp.swapaxes(b_s[:, :n], 0, 1)[:, :, None]
    return (u.reshape(b, c, n, A_GROUPS, A_GROUP) * mixed).reshape(b, t, A_WIDTH)


def short_conv(xin, buf_prev, taps):
    t = xin.shape[1]
    buf = jnp.concatenate([buf_prev.astype(xin.dtype), xin], axis=1)
    y = taps[0] * buf[:, 0:t]
    for j in range(1, CONV_W):
        y = y + taps[j] * buf[:, j:j + t]
    return y, buf[:, t:]


def mix_layer(x, buf_prev, w_in, w_s, b_s, v_g, v_b, conv_w, w_out):
    b, t = x.shape[:2]
    u, v, gate_b, gate_c, h = jnp.split(x @ w_in, MIX_SPLITS, axis=-1)
    vn = layer_norm(v.reshape(b, t, A_GROUPS, A_GROUP), v_g, v_b)
    a_out = sgu(u, vn, w_s, b_s)
    if buf_prev is None:
        buf_prev = jnp.zeros((b, CONV_W - 1, B_WIDTH), x.dtype)
    conv_out, buf_new = short_conv(gate_c * h, buf_prev, conv_w)
    y = jnp.concatenate([a_out, gate_b * conv_out], axis=-1) @ w_out
    return y, buf_new, vn


def dsa_project(x, w_in):
    b, t = x.shape[:2]
    q, k, v, qi, ki, wi = jnp.split(x @ w_in, ATTN_SPLITS, axis=-1)
    return (q.reshape(b, t, N_KV_HEADS, GQA_GROUP, HEAD_DIM),
            k.reshape(b, t, N_KV_HEADS, HEAD_DIM),
            v.reshape(b, t, N_KV_HEADS, HEAD_DIM),
            qi.reshape(b, t, IDX_HEADS, IDX_DIM), ki, wi)


def index_scores(qi, wi, ki, qpos, kpos):
    s = jax.nn.relu(jnp.einsum('bthd,bsd->bths', qi, ki).astype(jnp.float32))
    sc = jnp.einsum('bths,bth->bts', s, wi.astype(jnp.float32)) * IDX_SCALE
    return jnp.where(kpos[None, None, :] <= qpos[None, :, None], sc, -jnp.inf)


def attend_selected(q, ksel, vsel, valid):
    b, t = q.shape[:2]
    s = jnp.einsum('btkgd,btnkd->btkgn', q, ksel).astype(jnp.float32) * ATTN_SCALE
    s = jnp.where(valid[:, :, None, None, :], s, -jnp.inf)
    p = jax.nn.softmax(s, axis=-1).astype(vsel.dtype)
    return jnp.einsum('btkgn,btnkd->btkgd', p, vsel).reshape(b, t, Q_W)


def dsa_prompt(x, w_in, w_out):
    b, s = x.shape[:2]
    q, k, v, qi, ki, wi = dsa_project(x, w_in)
    topk = min(TOPK_MAX, s // 4)
    kpos = jnp.arange(s)

    def block(i):
        t0 = i * QBLOCK
        sl = lambda a: lax.dynamic_slice_in_dim(a, t0, QBLOCK, axis=1)
        qpos = t0 + jnp.arange(QBLOCK)
        sc = index_scores(sl(qi), sl(wi), ki, qpos, kpos)
        _, idx = lax.top_k(sc, topk)
        valid = idx <= qpos[None, :, None]
        return attend_selected(sl(q), gather_rows(k, idx), gather_rows(v, idx), valid)

    o = lax.map(block, jnp.arange(s // QBLOCK))
    o = jnp.swapaxes(o, 0, 1).reshape(b, s, Q_W)
    return o @ w_out, k, v, ki


def dsa_sample(x, cache_k, cache_v, cache_ki, page_table, w_in, w_out):
    b, t = x.shape[:2]
    q, k, v, qi, ki, wi = dsa_project(x, w_in)
    past = page_table.shape[1] * PAGE_SIZE
    ki_past = cache_ki[page_table].reshape(b, past, IDX_DIM)
    ki_all = jnp.concatenate([ki_past.astype(ki.dtype), ki], axis=1)
    n_keys = past + t
    topk = min(TOPK_MAX, n_keys // 4)
    qpos = past + jnp.arange(t)
    sc = index_scores(qi, wi, ki_all, qpos, jnp.arange(n_keys))
    _, idx = lax.top_k(sc, topk)
    valid = idx <= qpos[None, :, None]
    in_past = (idx < past)[..., None, None]
    pidx = jnp.minimum(idx, past - 1)
    phys = gather_rows(page_table, pidx // PAGE_SIZE)
    off = pidx % PAGE_SIZE
    nidx = jnp.clip(idx - past, 0, t - 1)
    ksel = jnp.where(in_past, cache_k[phys, off].astype(k.dtype), gather_rows(k, nidx))
    vsel = jnp.where(in_past, cache_v[phys, off].astype(v.dtype), gather_rows(v, nidx))
    o = attend_selected(q, ksel, vsel, valid)
    return o @ w_out, k, v, ki


def mem_kv(mem, w_k, w_v):
    b, m = mem.shape[:2]
    return ((mem @ w_k).reshape(b, m, MEM_HEADS, MEM_HEAD_DIM),
            (mem @ w_v).reshape(b, m, MEM_HEADS, MEM_HEAD_DIM))


def mem_cross(x, mk, mv, w_q, w_o):
    b, t = x.shape[:2]
    q = (x @ w_q).reshape(b, t, MEM_HEADS, MEM_HEAD_DIM)
    s = jnp.einsum('bthd,bmhd->bhtm', q, mk.astype(q.dtype)).astype(jnp.float32) * MEM_SCALE
    p = jax.nn.softmax(s, axis=-1).astype(x.dtype)
    o = jnp.einsum('bhtm,bmhd->bthd', p, mv.astype(x.dtype)).reshape(b, t, MEM_W)
    return o @ w_o


def setup_inputs(seed: int = 0) -> dict:
    key = jax.random.key(seed)
    ks = jax.random.split(key, 32)
    f32 = jnp.float32

    def nrm(k, shape, scale=1.0):
        return jax.random.normal(k, shape, f32) * scale

    n_pages = PAST_LEN // PAGE_SIZE
    n_used = DEC_BATCH * n_pages
    n_pool = (5 * n_used + 3) // 4
    page_table = jax.random.permutation(ks[0], n_pool)[:n_used].reshape(DEC_BATCH, n_pages).astype(jnp.int32)
    dsc = D_MODEL ** -0.5
    return {
        'x_prompt': nrm(ks[1], (BATCH, SEQ, D_MODEL)),
        'x_sample': nrm(ks[2], (DEC_BATCH, DEC_SEQ, D_MODEL)),
        'cache_attn_k': nrm(ks[3], (N_ATTN_LAYERS, n_pool, PAGE_SIZE, N_KV_HEADS, HEAD_DIM)),
        'cache_attn_v': nrm(ks[4], (N_ATTN_LAYERS, n_pool, PAGE_SIZE, N_KV_HEADS, HEAD_DIM)),
        'cache_idx_k': nrm(ks[5], (N_ATTN_LAYERS, n_pool, PAGE_SIZE, IDX_DIM)),
        'state_conv': nrm(ks[6], (N_MIX_LAYERS, DEC_BATCH, CONV_W - 1, B_WIDTH)),
        'cache_mem_k': nrm(ks[7], (DEPTH, DEC_BATCH, MEM_LEN, MEM_HEADS, MEM_HEAD_DIM)),
        'cache_mem_v': nrm(ks[8], (DEPTH, DEC_BATCH, MEM_LEN, MEM_HEADS, MEM_HEAD_DIM)),
        'page_table': page_table,
        'mem_prompt': nrm(ks[9], (BATCH, MEM_LEN, D_MODEL)),
        'w_in_mix': nrm(ks[10], (N_MIX_LAYERS, D_MODEL, MIX_IN), dsc),
        'sgu_w': nrm(ks[11], (N_MIX_LAYERS, A_GROUPS, CHUNK, CHUNK), CHUNK ** -0.5),
        'sgu_b': 1.0 + nrm(ks[12], (N_MIX_LAYERS, A_GROUPS, CHUNK), 0.01),
        'sgu_ln_g': 1.0 + nrm(ks[13], (N_MIX_LAYERS, A_GROUPS, A_GROUP), 0.02),
        'sgu_ln_b': nrm(ks[14], (N_MIX_LAYERS, A_GROUPS, A_GROUP), 0.02),
        'conv_w': nrm(ks[15], (N_MIX_LAYERS, CONV_W, B_WIDTH), CONV_W ** -0.5),
        'w_out_mix': nrm(ks[16], (N_MIX_LAYERS, D_MODEL, D_MODEL), dsc * BETA),
        'w_in_attn': nrm(ks[17], (N_ATTN_LAYERS, D_MODEL, ATTN_IN), dsc),
        'w_out_attn': nrm(ks[18], (N_ATTN_LAYERS, Q_W, D_MODEL), Q_W ** -0.5 * BETA),
        'w_mem_q': nrm(ks[19], (DEPTH, D_MODEL, MEM_W), dsc),
        'w_mem_k': nrm(ks[20], (DEPTH, D_MODEL, MEM_W), dsc),
        'w_mem_v': nrm(ks[21], (DEPTH, D_MODEL, MEM_W), dsc),
        'w_mem_o': nrm(ks[22], (DEPTH, MEM_W, D_MODEL), MEM_W ** -0.5 * BETA),
        'ffn_w_gu': nrm(ks[23], (DEPTH, 2, D_MODEL, 2 * D_FF), dsc),
        'ffn_w_down': nrm(ks[24], (DEPTH, 2, D_FF, D_MODEL), D_FF ** -0.5 * BETA),
        'ln_g': 1.0 + nrm(ks[25], (DEPTH, 4, D_MODEL), 0.02),
        'ln_b': nrm(ks[26], (DEPTH, 4, D_MODEL), 0.02),
    }


def reference(x_prompt, x_sample, cache_attn_k, cache_attn_v, cache_idx_k, state_conv, cache_mem_k, cache_mem_v,
              page_table, mem_prompt, w_in_mix, sgu_w, sgu_b, sgu_ln_g, sgu_ln_b, conv_w, w_out_mix,
              w_in_attn, w_out_attn, w_mem_q, w_mem_k, w_mem_v, w_mem_o, ffn_w_gu, ffn_w_down, ln_g, ln_b):
    hp = x_prompt
    hs = x_sample
    attn_k_p, attn_v_p, idx_k_p, conv_p, mem_k_p, mem_v_p = [], [], [], [], [], []
    attn_k_s, attn_v_s, idx_k_s, conv_s, sgu_v_s = [], [], [], [], []
    for l in range(DEPTH):
        j = l // 2
        hp = post_ln(hp, 0.5 * swiglu(hp, ffn_w_gu[l, 0], ffn_w_down[l, 0]), ln_g[l, 0], ln_b[l, 0])
        hs = post_ln(hs, 0.5 * swiglu(hs, ffn_w_gu[l, 0], ffn_w_down[l, 0]), ln_g[l, 0], ln_b[l, 0])
        if l % 2 == 0:
            mix_w = (w_in_mix[j], sgu_w[j], sgu_b[j], sgu_ln_g[j], sgu_ln_b[j], conv_w[j], w_out_mix[j])
            yp, cbuf_p, _ = mix_layer(hp, None, *mix_w)
            ys, cbuf_s, vn_s = mix_layer(hs, state_conv[j], *mix_w)
            conv_p.append(cbuf_p)
            conv_s.append(cbuf_s)
            sgu_v_s.append(vn_s)
        else:
            yp, kp, vp, kip = dsa_prompt(hp, w_in_attn[j], w_out_attn[j])
            ys, ksn, vsn, kisn = dsa_sample(hs, cache_attn_k[j], cache_attn_v[j], cache_idx_k[j], page_table,
                                            w_in_attn[j], w_out_attn[j])
            attn_k_p.append(kp)
            attn_v_p.append(vp)
            idx_k_p.append(kip)
            attn_k_s.append(ksn)
            attn_v_s.append(vsn)
            idx_k_s.append(kisn)
        hp = post_ln(hp, yp, ln_g[l, 1], ln_b[l, 1])
        hs = post_ln(hs, ys, ln_g[l, 1], ln_b[l, 1])
        mkp, mvp = mem_kv(mem_prompt, w_mem_k[l], w_mem_v[l])
        mem_k_p.append(mkp)
        mem_v_p.append(mvp)
        hp = post_ln(hp, mem_cross(hp, mkp, mvp, w_mem_q[l], w_mem_o[l]), ln_g[l, 2], ln_b[l, 2])
        hs = post_ln(hs, mem_cross(hs, cache_mem_k[l], cache_mem_v[l], w_mem_q[l], w_mem_o[l]), ln_g[l, 2], ln_b[l, 2])
        hp = post_ln(hp, 0.5 * swiglu(hp, ffn_w_gu[l, 1], ffn_w_down[l, 1]), ln_g[l, 3], ln_b[l, 3])
        hs = post_ln(hs, 0.5 * swiglu(hs, ffn_w_gu[l, 1], ffn_w_down[l, 1]), ln_g[l, 3], ln_b[l, 3])
    y_prompt = hp
    y_sample = hs
    new_attn_k_prompt = jnp.stack(attn_k_p)
    new_attn_v_prompt = jnp.stack(attn_v_p)
    new_idx_k_prompt = jnp.stack(idx_k_p)
    new_conv_prompt = jnp.stack(conv_p)
    new_mem_k_prompt = jnp.stack(mem_k_p)
    new_mem_v_prompt = jnp.stack(mem_v_p)
    new_attn_k_sample = jnp.stack(attn_k_s)
    new_attn_v_sample = jnp.stack(attn_v_s)
    new_idx_k_sample = jnp.stack(idx_k_s)
    new_conv_sample = jnp.stack(conv_s)
    new_sgu_v_sample = jnp.stack(sgu_v_s)
    return (y_prompt, y_sample, new_attn_k_prompt, new_attn_v_prompt, new_idx_k_prompt, new_conv_prompt,
            new_mem_k_prompt, new_mem_v_prompt, new_attn_k_sample, new_attn_v_sample, new_idx_k_sample,
            new_conv_sample, new_sgu_v_sample)
```

```python
import numpy as np
from contextlib import ExitStack
import concourse.bass as bass
import concourse.mybir as mybir
from concourse.bass_utils import run_bass_kernel_spmd

F32 = mybir.dt.float32
BF16 = mybir.dt.bfloat16
I32 = mybir.dt.int32
ALU = mybir.AluOpType
AF = mybir.ActivationFunctionType
AX = mybir.AxisListType

import os
NCORES = int(os.environ.get('K_CORES', 8))
D = 2048
NCH = 16
DFF = 5632
T = 512
SEQ = 2048
NT = SEQ // T
NS = 16
NPOOL = int(os.environ.get('K_NPOOL', 2560))
SKIP = os.environ.get('K_SKIP', '').split(',')
ALPHA = 4.0 ** 0.25
C_FFN = 0.5 / ALPHA
C_ONE = 1.0 / ALPHA
EPS2 = 1e-5 / (ALPHA * ALPHA)
SCALE = 128.0 ** -0.5
NEG = -1.0e30
SW = 2176

NBF = 96
NF = 26
NW = 4
X0, KT0, V0, KI0, MEM0, SB0 = 0, 16, 32, 48, 52, 60
H0, SF0 = 0, 16


class Ctx:
    def __init__(self, nc, es):
        self.nc = nc
        self.es = es
        self.E = {"pe": nc.tensor, "act": nc.scalar, "dve": nc.vector, "pool": nc.gpsimd, "sp": nc.sync}
        self.sem = {}
        self.cnt = {}
        for e in ["pe", "act", "dve", "pool"]:
            self.sem[e] = es.enter_context(nc.semaphore("s_" + e))
            self.cnt[e] = 0
        self.waited = {}
        self.reg = {}
        self.pend = {e: [] for e in self.E}
        self.dsem = {}
        self.dcnt = {}
        self.ninst = 0
        self.dead = False
        self.pre = int(os.environ.get('K_PRE', 99))

    def mark(self, k):
        if k > self.pre:
            self.dead = True

    def _semh(self, name):
        return self.sem[name] if name in self.sem else self.dsem[name]

    def _wait(self, eng, tok):
        name, val = tok
        if name in self.dcnt:
            val = max(val, 16 * self.dcnt[name])
        k = (eng, name)
        if self.waited.get(k, 0) >= val:
            return
        self.waited[k] = val
        self.E[eng].wait_ge(self._semh(name), val)
        self.ninst += 1

    def _deps(self, eng, reads, writes):
        toks = []
        for k in reads:
            r = self.reg.get(k)
            if r and r[0] is not None:
                toks.append(r[0])
        for k in writes:
            r = self.reg.get(k)
            if r:
                if r[0] is not None:
                    toks.append(r[0])
                toks.extend(r[1].items())
        for t in toks:
            self._wait(eng, t)

    def _check_pending(self, eng, reads, writes):
        for e, lst in self.pend.items():
            if e == eng or not lst:
                continue
            for (k, w) in lst:
                if (k in writes) or (w and k in reads):
                    raise RuntimeError(f"pending unsynced access on {k} by {e}, needed by {eng}")

    def _record(self, tok, reads, writes):
        for k in reads:
            r = self.reg.setdefault(k, [None, {}])
            if r[1].get(tok[0], 0) < tok[1]:
                r[1][tok[0]] = tok[1]
        for k in writes:
            self.reg[k] = [tok, {}]

    def op(self, eng, fn, reads=(), writes=(), inc=True):
        if self.dead:
            return None
        pr = tuple(k for k in reads if isinstance(k, tuple) and k[0] == "P")
        reads = tuple(k for k in reads if not (isinstance(k, tuple) and k[0] == "P"))
        writes = tuple(writes) + pr
        self._check_pending(eng, reads, writes)
        self._deps(eng, reads, writes)
        ins = fn(self.E[eng])
        self.ninst += 1
        if inc:
            self.cnt[eng] += 1
            ins.then_inc(self.sem[eng], 1)
            tok = (eng, self.cnt[eng])
            for (k, w) in self.pend[eng]:
                self._record(tok, () if w else (k,), (k,) if w else ())
            self.pend[eng] = []
            self._record(tok, reads, writes)
            return tok
        for k in reads:
            self.pend[eng].append((k, False))
        for k in writes:
            self.pend[eng].append((k, True))
        return None

    def dma(self, q, semname, fn, reads=(), writes=()):
        if self.dead:
            return None
        if semname not in self.dsem:
            self.dsem[semname] = self.es.enter_context(self.nc.semaphore("d_" + semname))
            self.dcnt[semname] = 0
        reads = tuple(reads)
        writes = tuple(writes)
        self._check_pending(q, reads, writes)
        self._deps(q, reads, writes)
        ins = fn(self.E[q])
        self.ninst += 1
        self.dcnt[semname] += 1
        ins.then_inc(self.dsem[semname], 16)
        tok = (semname, 16 * self.dcnt[semname])
        self._record(tok, reads, writes)
        return tok

    def finish(self, eng="sp"):
        for name, c in self.dcnt.items():
            self._wait(eng, (name, 16 * c))
        for e in ["pe", "act", "dve", "pool"]:
            if self.cnt[e]:
                self._wait(eng, (e, self.cnt[e]))


def Bk(*slots):
    out = []
    for s in slots:
        if isinstance(s, range):
            out.extend(("B", i) for i in s)
        else:
            out.append(("B", s))
    return out


def Fk(*slots):
    out = []
    for s in slots:
        if isinstance(s, range):
            out.extend(("F", i) for i in s)
        else:
            out.append(("F", s))
    return out


def Pk(*banks):
    return [("P", b) for b in banks]


def build_program():
    nc = bass.Bass("TRN2", target_bir_lowering=False)

    def din(name, shape, dt=F32):
        return nc.dram_tensor(name, shape, dt, kind="ExternalInput").ap()

    def dout(name, shape, dt=F32):
        return nc.dram_tensor(name, shape, dt, kind="ExternalOutput").ap()

    xp = din("xp", [SEQ, D]); xs = din("xs", [NS, D])
    cak = din("cak", [NPOOL * 128, 512]); cav = din("cav", [NPOOL * 128, 512]); cik = din("cik", [NPOOL * 128, 64])
    stc = din("stc", [NS * 2, 1024]); cmk = din("cmk", [2, NS, 256, 512]); cmv = din("cmv", [2, NS, 256, 512])
    ptab = din("ptab", [1, NS * 16], I32); memp = din("memp", [256, D])
    w_in_mix = din("w_in_mix", [D, 5120]); sgu_w = din("sgu_w", [8, 128, 128]); sgu_b = din("sgu_b", [1, 1024])
    sgu_g = din("sgu_g", [8, 128]); sgu_bb = din("sgu_bb", [8, 128]); conv_w = din("conv_w", [24, 128])
    w_out_mix = din("w_out_mix", [D, D]); w_in_attn = din("w_in_attn", [D, 4176]); w_out_attn = din("w_out_attn", [D, D])
    w_mem_q = din("w_mem_q", [2, D, 512]); w_mem_k = din("w_mem_k", [2, D, 512]); w_mem_v = din("w_mem_v", [2, D, 512])
    w_mem_o = din("w_mem_o", [2, 512, D]); w_gu = din("w_gu", [4, D, 2 * DFF]); w_dn = din("w_dn", [4, DFF, D])
    ln_g = din("ln_g", [128, 128]); ln_b = din("ln_b", [128, 128])
    c_ident = din("c_ident", [128, 128]); c_triu = din("c_triu", [128, 128]); c_cmask = din("c_cmask", [128, 128])
    c_eye16 = din("c_eye16", [16, 256])

    y_p = dout("y_p", [SEQ, D]); y_s = dout("y_s", [NS, D])
    ak_p = dout("ak_p", [SEQ, 512]); av_p = dout("av_p", [SEQ, 512]); ik_p = dout("ik_p", [SEQ, 64])
    cv_p = dout("cv_p", [2, 1024]); mk_p = dout("mk_p", [2, 256, 512]); mv_p = dout("mv_p", [2, 256, 512])
    ak_s = dout("ak_s", [NS, 512]); av_s = dout("av_s", [NS, 512]); ik_s = dout("ik_s", [NS, 64])
    cv_s = dout("cv_s", [NS, 2, 1024]); sv_s = dout("sv_s", [NS, 1024])

    with ExitStack() as es:
        def sb(name, shape, dt=F32):
            return es.enter_context(nc.sbuf_tensor(name, shape, dt))

        BFP = sb("BFP", [128, NBF, 512], BF16)
        FP = sb("FP", [128, NF, 512], F32)
        WP = sb("WP", [128, NW, 4096], BF16)
        PS = [es.enter_context(nc.psum_tensor(f"ps{i}", [128, 512], F32)) for i in range(8)]
        ident = sb("ident", [128, 128]); idb = sb("idb", [128, 128], BF16)
        triu = sb("triu", [128, 128]); cmask = sb("cmask", [128, 128])
        ones_f = sb("ones_f", [128, 128]); ones_b = sb("ones_b", [128, 128], BF16)
        lngT = sb("lngT", [128, 128]); lnbT = sb("lnbT", [128, 128])
        convT = sb("convT", [128, 24]); sguT = sb("sguT", [128, 16])
        swT = sb("swT", [128, 8, 128], BF16); sgE = sb("sgE", [128, 8, 128])
        halo = sb("halo", [128, 8, 2])
        idxall = sb("idxall", [128, NS * 16], I32)
        wi_sb = sb("wi_sb", [128, 4, 16]); m8 = sb("m8", [128, 8])
        eye16 = sb("eye16", [16, 256]); w00 = sb("w00", [16, 8]); b00 = sb("b00", [16, 8])
        small = sb("small", [128, 64])
        hTs = sb("hTs", [128, 16, NS]); xTs = sb("xTs", [128, 16, NS], BF16); actTs = sb("actTs", [128, 22, NS], BF16)
        sA = sb("sA", [128, 16, NS], BF16)
        sBt = sb("sBt", [128, 16, NS], BF16)
        sF = sb("sF", [128, 24, NS])
        lnS = sb("lnS", [128, 4, NS])
        qiTs = sb("qiTs", [64, 16, NS], BF16); wiTs = sb("wiTs", [16, NS]); wsel = sb("wsel", [16, 16, 16])
        kiTs_new = sb("kiTs_new", [128, NS], BF16)
        maskTs = sb("maskTs", [128, 17, NS], BF16); pTs = sb("pTs", [128, 17, 16], BF16); pTf = sb("pTf", [128, 17, 16])
        stT = sb("stT", [128, 8, 32])

        es.enter_context(nc.Block())
        es.enter_context(nc.allow_non_contiguous_dma(reason="tiny strided state outputs"))
        c = Ctx(nc, es)

        def bf(s):
            return BFP[:, s, :]

        def bfr(s0, n):
            return BFP[:, s0:s0 + n, :].rearrange("p s c -> p (s c)")

        def ff(s):
            return FP[:, s, :]

        def ffr(s0, n):
            return FP[:, s0:s0 + n, :].rearrange("p s c -> p (s c)")

        sTok = ffr(SF0 + 1, 4)[0:16, :]; sTokK = Fk(range(SF0 + 1, SF0 + 5))
        tA = ffr(SF0, 4)[0:16, :]; tAk = Fk(range(SF0, SF0 + 4))
        tB = ffr(SF0 + 4, 4)[0:16, :]; tBk = Fk(range(SF0 + 4, SF0 + 8))
        rot = {}

        def nxt(name, lst):
            i = rot.get(name, 0)
            rot[name] = i + 1
            return lst[i % len(lst)]

        NSL = 420
        wtw = [nc.dram_tensor(f"wtwin{i}", [105, 128, 4096], BF16, kind="Internal").ap() for i in range(4)]
        stored = {}

        def wload(w2d, r0, kch, c0, ncols, dst_col=0, slot=None, width=None, twin=True):
            width = width or ncols
            if slot is None:
                slot = nxt("w", list(range(NW)))
            view = WP[:, slot, 0:kch * width].rearrange("p (k n) -> p k n", k=kch)
            dst = view[:, :, dst_col:dst_col + ncols]
            key = (w2d.tensor.name, int(w2d.offset), r0, kch, c0, ncols)
            if twin and key in stored:
                idx = stored[key]
                src = wtw[idx // 105][idx % 105, :, 0:kch * ncols].rearrange("p (k n) -> p k n", k=kch)
                c.dma("pool", f"w{slot}", lambda e: e.dma_start(out=dst, in_=src), reads=[("D", idx)], writes=[("W", slot)])
                return view, ("W", slot), slot
            src = w2d[r0:r0 + kch * 128, c0:c0 + ncols].rearrange("(k p) n -> p k n", p=128)
            c.dma("pool", f"w{slot}", lambda e: e.dma_start(out=dst, in_=src), writes=[("W", slot)])
            if twin and len(stored) < NSL:
                idx = len(stored)
                stored[key] = idx
                tdst = wtw[idx // 105][idx % 105, :, 0:kch * ncols].rearrange("p (k n) -> p k n", k=kch)
                c.dma("sp", "wst", lambda e: e.dma_start(out=tdst, in_=dst), reads=[("W", slot)], writes=[("D", idx)])
            return view, ("W", slot), slot

        def mm(ps_ap, pairs, reads, pkey):
            n = len(pairs)
            for i, (l, r) in enumerate(pairs):
                c.op("pe", lambda e: e.matmul(ps_ap, lhsT=l, rhs=r, start=(i == 0), stop=(i == n - 1)),
                     reads=reads if i == 0 else (), writes=[pkey], inc=(i == n - 1))

        def tr(ps_ap, in_ap, idt, reads, pkey, inc=True):
            c.op("pe", lambda e: e.transpose(ps_ap, in_ap, idt), reads=reads, writes=[pkey], inc=inc)

        class Stream:
            pass

        P = Stream(); P.T = T; P.name = "P"
        P.hT = [ff(H0 + i) for i in range(16)]; P.hk = [("F", H0 + i) for i in range(16)]
        P.xT = [bf(X0 + i) for i in range(16)]; P.xk = [("B", X0 + i) for i in range(16)]
        P.aT = [bf(SB0 + i) for i in range(22)]; P.ak = [("B", SB0 + i) for i in range(22)]
        S = Stream(); S.T = NS; S.name = "S"
        S.hT = [hTs[:, i, :] for i in range(16)]; S.hk = [("hTs", i) for i in range(16)]
        S.xT = [xTs[:, i, :] for i in range(16)]; S.xk = [("xTs", i) for i in range(16)]
        S.aT = [actTs[:, i, :] for i in range(22)]; S.ak = [("actTs", i) for i in range(22)]

        def ln_stats(st, m, s1b, s2b, first, last):
            Tn = st.T
            sq = nxt("sq" + st.name, [SF0 + 8, SF0 + 9]) if st is P else None
            if st is P:
                sqap, sqk = ff(sq)[:, :Tn], ("F", sq)
            else:
                j = nxt("sqS", [0, 1])
                sqap, sqk = lnS[:, 2 + j, :], ("lnS", 2 + j)
            c.op("act", lambda e: e.activation(out=sqap, in_=st.hT[m], func=AF.Square), reads=[st.hk[m]], writes=[sqk])
            c.op("pe", lambda e: e.matmul(PS[s1b][:, :Tn], lhsT=ones_f[:], rhs=st.hT[m], start=first, stop=last),
                 reads=[st.hk[m], "ones_f"], writes=[("P", s1b)], inc=False)
            c.op("pe", lambda e: e.matmul(PS[s2b][:, :Tn], lhsT=ones_f[:], rhs=sqap, start=first, stop=last),
                 reads=[sqk], writes=[("P", s2b)], inc=True)

        def ln_finish(st, li, s1b, s2b):
            Tn = st.T
            if st is P:
                mean, mk = ff(SF0 + 5)[:, :Tn], ("F", SF0 + 5)
                rstd, rk = ff(SF0 + 6)[:, :Tn], ("F", SF0 + 6)
                tmps = [(ff(SF0 + 8)[:, :Tn], ("F", SF0 + 8)), (ff(SF0 + 9)[:, :Tn], ("F", SF0 + 9))]
            else:
                mean, mk = lnS[:, 0, :], ("lnS", 0)
                rstd, rk = lnS[:, 1, :], ("lnS", 1)
                tmps = [(lnS[:, 2, :], ("lnS", 2)), (lnS[:, 3, :], ("lnS", 3))]
            c.op("dve", lambda e: e.tensor_scalar(out=mean, in0=PS[s1b][:, :Tn], scalar1=1.0 / D, scalar2=None, op0=ALU.mult),
                 reads=Pk(s1b), writes=[mk])
            t0, t0k = tmps[0]
            c.op("dve", lambda e: e.tensor_tensor(out=t0, in0=mean, in1=mean, op=ALU.mult), reads=[mk], writes=[t0k])
            c.op("dve", lambda e: e.scalar_tensor_tensor(out=rstd, in0=PS[s2b][:, :Tn], scalar=1.0 / D, in1=t0,
                                                         op0=ALU.mult, op1=ALU.subtract), reads=Pk(s2b) + [t0k], writes=[rk])
            c.op("dve", lambda e: e.tensor_scalar(out=rstd, in0=rstd, scalar1=EPS2, scalar2=None, op0=ALU.add), reads=[rk], writes=[rk])
            c.op("act", lambda e: e.activation(out=rstd, in_=rstd, func=AF.Sqrt), reads=[rk], writes=[rk])
            c.op("dve", lambda e: e.reciprocal(out=rstd, in_=rstd), reads=[rk], writes=[rk])
            for m in range(16):
                tp, tk = tmps[m % 2]
                gcol = lngT[:, li * 16 + m:li * 16 + m + 1]
                bcol = lnbT[:, li * 16 + m:li * 16 + m + 1]
                c.op("dve", lambda e: e.tensor_tensor(out=tp, in0=st.hT[m], in1=mean, op=ALU.subtract), reads=[st.hk[m], mk], writes=[tk])
                c.op("dve", lambda e: e.scalar_tensor_tensor(out=tp, in0=tp, scalar=gcol, in1=rstd, op0=ALU.mult, op1=ALU.mult),
                     reads=[tk, rk, "lngT"], writes=[tk])
                c.op("act", lambda e: e.activation(out=st.hT[m], in_=tp, func=AF.Identity, bias=bcol), reads=[tk, "lnbT"], writes=[st.hk[m]])
                c.op("act", lambda e: e.activation(out=st.xT[m], in_=tp, func=AF.Identity, bias=bcol), reads=[tk, "lnbT"], writes=[st.xk[m]])

        def out_linear(streams, li, cres, slab_fn, in_aps, n_kc):
            banks = {"P": [0, 1, 2], "S": [3]}
            sbank = {"P": (4, 5), "S": (6, 7)}
            pending = []

            def flush(keep):
                while len(pending) > keep:
                    st, m = pending.pop(0)
                    ln_stats(st, m, sbank[st.name][0], sbank[st.name][1], first=(m == 0), last=(m == 15))
            for m in range(16):
                view, wk, co = slab_fn(m)
                for st in streams:
                    b = nxt("ob" + st.name, banks[st.name])
                    aps, aks = in_aps[st.name]
                    mm(PS[b][:, :st.T], [(view[:, kc, co:co + 128], aps[kc]) for kc in range(n_kc)], [wk] + list(aks), ("P", b))
                    c.op("dve", lambda e: e.scalar_tensor_tensor(out=st.hT[m], in0=PS[b][:, :st.T], scalar=cres, in1=st.hT[m],
                                                                 op0=ALU.mult, op1=ALU.add), reads=Pk(b) + [st.hk[m]], writes=[st.hk[m]])
                    pending.append((st, m))
                flush(2 * len(streams))
            flush(0)
            for st in streams:
                ln_finish(st, li, sbank[st.name][0], sbank[st.name][1])

        def ffn(streams, widx, li):
            wg = w_gu[widx]; wd = w_dn[widx]
            for half in range(2):
                for sl in range(11):
                    j0 = half * 22 + sl * 2
                    gv, gk, _ = wload(wg, 0, 16, j0 * 128, 256)
                    uv, uk, _ = wload(wg, 0, 16, DFF + j0 * 128, 256)
                    for jj in range(2):
                        jl = sl * 2 + jj
                        for st in streams:
                            Tn = st.T
                            if st is P:
                                bg, bu = nxt("gu", [(0, 1), (2, 3), (4, 5)])
                                pg, pu = PS[bg][:, :Tn], PS[bu][:, :Tn]
                                kg, ku = ("P", bg), ("P", bu)
                                fs = nxt("silu", [SF0 + 8, SF0 + 9])
                                tmp, tmpk = ff(fs)[:, :Tn], ("F", fs)
                            else:
                                pg, pu = PS[6][:, 0:Tn], PS[7][:, 0:Tn]
                                kg, ku = ("P", 6), ("P", 7)
                                tmp, tmpk = lnS[:, 2, :], ("lnS", 2)
                            mm(pg, [(gv[:, kc, jj * 128:(jj + 1) * 128], st.xT[kc]) for kc in range(16)], [gk] + st.xk, kg)
                            mm(pu, [(uv[:, kc, jj * 128:(jj + 1) * 128], st.xT[kc]) for kc in range(16)], [uk] + st.xk, ku)
                            c.op("act", lambda e: e.activation(out=tmp, in_=pg, func=AF.Silu), reads=[kg], writes=[tmpk])
                            c.op("dve", lambda e: e.tensor_tensor(out=st.aT[jl], in0=tmp, in1=pu, op=ALU.mult), reads=[tmpk, ku], writes=[st.ak[jl]])
                last = (half == 1)
                banks = {"P": [0, 1, 2], "S": [3]}
                pending = []

                def flush(keep):
                    while len(pending) > keep:
                        st, m = pending.pop(0)
                        ln_stats(st, m, 4 if st is P else 6, 5 if st is P else 7, first=(m == 0), last=(m == 15))
                for m in range(16):
                    dv, dk, _ = wload(wd, half * 2816, 22, m * 128, 128)
                    for st in streams:
                        b = nxt("ob" + st.name, banks[st.name])
                        mm(PS[b][:, :st.T], [(dv[:, kc, :], st.aT[kc]) for kc in range(22)], [dk] + st.ak, ("P", b))
                        c.op("dve", lambda e: e.scalar_tensor_tensor(out=st.hT[m], in0=PS[b][:, :st.T], scalar=C_FFN, in1=st.hT[m],
                                                                     op0=ALU.mult, op1=ALU.add), reads=Pk(b) + [st.hk[m]], writes=[st.hk[m]])
                        if last:
                            pending.append((st, m))
                    if last:
                        flush(2 * len(streams))
                if last:
                    flush(0)
            for st in streams:
                ln_finish(st, li, 4 if st is P else 6, 5 if st is P else 7)

        def proj_fm(streams, w2d, c0, nchunks, sink, dup64=False):
            ci = 0
            while ci < nchunks:
                n = min(2, nchunks - ci)
                view, wk, _ = wload(w2d, 0, 16, c0 + ci * 128, n * 128)
                for jj in range(n):
                    for st in streams:
                        b = nxt("pf" + st.name, [0, 1, 2] if st is P else [3])
                        mm(PS[b][:, :st.T], [(view[:, kc, jj * 128:(jj + 1) * 128], st.xT[kc]) for kc in range(16)], [wk] + st.xk, ("P", b))
                        sink(st, ci + jj, PS[b][:, :st.T], ("P", b))
                ci += n

        def proj_tm(st, w2d, c0, ncols, nsub, sink, view=None, wk=None, voff=0):
            if view is None:
                view, wk, _ = wload(w2d, 0, 16, c0, ncols)
            for sub in range(nsub):
                b = nxt("pt", [4, 5])
                M = 128 if st is P else NS
                lhs = [(st.xT[kc][:, sub * 128:(sub + 1) * 128] if st is P else st.xT[kc]) for kc in range(16)]
                mm(PS[b][0:M, 0:ncols], [(lhs[kc], view[:, kc, voff:voff + ncols]) for kc in range(16)], [wk] + st.xk, ("P", b))
                sink(sub, PS[b][0:M, 0:ncols], ("P", b))

        for (dst, src, nm) in [(ident, c_ident, "ident"), (triu, c_triu, "triu"), (cmask, c_cmask, "cmask"), (eye16, c_eye16, "eye16")]:
            c.dma("sp", "pre", lambda e: e.dma_start(out=dst[:], in_=src), writes=[nm])
        c.dma("pool", "prec", lambda e: e.dma_start(out=idb[:], in_=c_ident), writes=["idb"])
        c.op("dve", lambda e: e.memset(ones_f[:], 1.0), writes=["ones_f"])
        c.op("dve", lambda e: e.memset(ones_b[:], 1.0), writes=["ones_b"])
        c.op("dve", lambda e: e.memset(halo[:], 0.0), writes=["halo"])
        c.mark(1)
        c.dma("sp", "pre", lambda e: e.dma_start(out=ff(SF0)[:, 0:128], in_=ln_g), writes=Fk(SF0))
        c.dma("sp", "pre", lambda e: e.dma_start(out=ff(SF0)[:, 128:256], in_=ln_b), writes=Fk(SF0))
        c.dma("sp", "pre", lambda e: e.dma_start(out=ff(SF0)[0:24, 256:384], in_=conv_w), writes=Fk(SF0))
        c.dma("sp", "pre", lambda e: e.dma_start(out=ff(SF0)[0:8, 384:512], in_=sgu_g), writes=Fk(SF0))
        c.dma("sp", "pre", lambda e: e.dma_start(out=ff(SF0)[8:16, 384:512], in_=sgu_bb), writes=Fk(SF0))
        tr(PS[0][:, 0:128], ff(SF0)[:, 0:128], ident[:], Fk(SF0) + ["ident"], ("P", 0), inc=False)
        tr(PS[0][:, 128:256], ff(SF0)[:, 128:256], ident[:], Fk(SF0), ("P", 0), inc=False)
        tr(PS[0][:, 256:280], ff(SF0)[0:24, 256:384], ident[0:24, 0:24], Fk(SF0), ("P", 0), inc=False)
        tr(PS[0][:, 280:296], ff(SF0)[0:16, 384:512], ident[0:16, 0:16], Fk(SF0), ("P", 0))
        c.op("dve", lambda e: e.tensor_copy(out=lngT[:], in_=PS[0][:, 0:128]), reads=Pk(0), writes=["lngT"])
        c.op("dve", lambda e: e.tensor_copy(out=lnbT[:], in_=PS[0][:, 128:256]), reads=Pk(0), writes=["lnbT"])
        c.op("dve", lambda e: e.tensor_copy(out=convT[:], in_=PS[0][:, 256:280]), reads=Pk(0), writes=["convT"])
        c.op("dve", lambda e: e.tensor_copy(out=sguT[:], in_=PS[0][:, 280:296]), reads=Pk(0), writes=["sguT"])
        c.mark(2)
        c.dma("sp", "pre", lambda e: e.dma_start(out=ffr(SF0 + 1, 2)[:, 0:1024].rearrange("p (g s) -> p g s", g=8),
                                                 in_=sgu_w.rearrange("g t s -> t g s")), writes=Fk(SF0 + 1, SF0 + 2))
        c.dma("sp", "pre", lambda e: e.dma_start(out=ffr(SF0 + 3, 2), in_=sgu_b.to_broadcast([128, 1024])), writes=Fk(SF0 + 3, SF0 + 4))
        for g in range(8):
            b = 1 + g // 4
            tr(PS[b][:, (g % 4) * 128:(g % 4 + 1) * 128], ffr(SF0 + 1, 2)[:, g * 128:(g + 1) * 128], ident[:], Fk(SF0 + 1, SF0 + 2) + ["ident"], ("P", b), inc=(g % 4 == 3))
        w32 = ffr(SF0 + 5, 2)
        for hb in range(2):
            c.op("dve", lambda e: e.tensor_tensor(out=w32[:, hb * 512:(hb + 1) * 512].rearrange("p (g t) -> p g t", g=4),
                                                  in0=PS[1 + hb][:].rearrange("p (g t) -> p g t", g=4),
                                                  in1=triu[:].unsqueeze(1).to_broadcast([128, 4, 128]), op=ALU.mult),
                 reads=Pk(1 + hb) + ["triu"], writes=Fk(SF0 + 5, SF0 + 6))
        c.op("act", lambda e: e.copy(out=swT[:].rearrange("p g t -> p (g t)"), in_=w32), reads=Fk(SF0 + 5, SF0 + 6), writes=["swT"])
        for hb in range(2):
            c.op("pe", lambda e: e.matmul(PS[3 + hb][:], lhsT=ones_f[:], rhs=w32[:, hb * 512:(hb + 1) * 512], start=True, stop=True),
                 reads=Fk(SF0 + 5, SF0 + 6) + ["ones_f"], writes=Pk(3 + hb))
        for g in range(8):
            c.op("dve", lambda e: e.scalar_tensor_tensor(out=sgE[:, g, :], in0=PS[3 + g // 4][:, (g % 4) * 128:(g % 4 + 1) * 128],
                                                         scalar=sguT[:, 8 + g:9 + g], in1=ffr(SF0 + 3, 2)[:, g * 128:(g + 1) * 128],
                                                         op0=ALU.mult, op1=ALU.add),
                 reads=Pk(3 + g // 4) + Fk(SF0 + 3, SF0 + 4) + ["sguT"], writes=["sgE"])
        c.mark(3)
        if "scal" not in SKIP:
            c.dma("sp", "pre", lambda e: e.dma_start(out=w00[:], in_=sgu_w[:, 0, 0:1].rearrange("g o -> o g").to_broadcast([16, 8])), writes=["w00"])
            c.dma("sp", "pre", lambda e: e.dma_start(out=b00[:], in_=sgu_b.rearrange("o (g t) -> o g t", g=8)[:, :, 0].to_broadcast([16, 8])), writes=["b00"])
        c.mark(4)
        c.dma("sp", "pre", lambda e: e.dma_start(out=idxall[:], in_=ptab.to_broadcast([128, NS * 16])), writes=["idxall"])
        c.op("pool", lambda e: e.iota(small[:, 0:1].bitcast(I32), pattern=[[0, 1]], base=0, channel_multiplier=1), writes=["small"])
        c.op("dve", lambda e: e.tensor_copy(out=small[:, 1:2], in_=small[:, 0:1].bitcast(I32)), reads=["small"], writes=["small"])
        ptf = ff(SF0 + 7)[:, 0:256]
        c.op("dve", lambda e: e.tensor_copy(out=ptf, in_=idxall[:]), reads=["idxall"], writes=Fk(SF0 + 7))
        c.op("dve", lambda e: e.tensor_scalar(out=ptf, in0=ptf, scalar1=128.0, scalar2=small[:, 1:2], op0=ALU.mult, op1=ALU.add),
             reads=Fk(SF0 + 7) + ["small"], writes=Fk(SF0 + 7))
        c.op("dve", lambda e: e.tensor_copy(out=idxall[:], in_=ptf), reads=Fk(SF0 + 7), writes=["idxall"])

        c.mark(5)
        memT = bfr(SB0, 8).rearrange("p (k m) -> p k m", k=16)
        memk = Bk(range(SB0, SB0 + 8))
        for mt in range(2):
            stg = ffr(SF0 + 1, 4)
            c.dma("sp", "mem", lambda e: e.dma_start(out=stg, in_=memp[mt * 128:(mt + 1) * 128, :]), writes=Fk(range(SF0 + 1, SF0 + 5)))
            for cg in range(4):
                b = nxt("pre_t", [0, 1])
                for q in range(4):
                    ch = cg * 4 + q
                    tr(PS[b][:, q * 128:(q + 1) * 128], stg[:, ch * 128:(ch + 1) * 128], ident[:], Fk(range(SF0 + 1, SF0 + 5)) + ["ident"], ("P", b), inc=(q == 3))
                c.op("act", lambda e: e.copy(out=memT[:, cg * 4:(cg + 1) * 4, mt * 128:(mt + 1) * 128],
                                             in_=PS[b][:].rearrange("p (q t) -> p q t", q=4)), reads=Pk(b), writes=memk)
        memTl = [memT[:, kc, :] for kc in range(16)]
        for l in range(2):
            mkT = bfr(MEM0 + l * 4, 2).rearrange("p (h m) -> p h m", h=4)
            mvv = BFP[:, MEM0 + l * 4 + 2:MEM0 + l * 4 + 4, :]
            for (wsrc, is_k) in [(w_mem_k[l], True), (w_mem_v[l], False)]:
                for cs in range(2):
                    view, wk, _ = wload(wsrc, 0, 16, cs * 256, 256, twin=False)
                    if is_k:
                        for jj in range(2):
                            h = cs * 2 + jj
                            b = nxt("pre_t", [0, 1])
                            mm(PS[b][:, 0:256], [(view[:, kc, jj * 128:(jj + 1) * 128], memTl[kc]) for kc in range(16)], [wk] + memk, ("P", b))
                            c.op("act", lambda e: e.copy(out=mkT[:, h, :], in_=PS[b][:, 0:256]), reads=Pk(b), writes=Bk(MEM0 + l * 4, MEM0 + l * 4 + 1))
                    for mt in range(2):
                        b = nxt("pre_t2", [2, 3])
                        mm(PS[b][:, 0:256], [(memT[:, kc, mt * 128:(mt + 1) * 128], view[:, kc, :]) for kc in range(16)], [wk] + memk, ("P", b))
                        so = nxt("pre_o", [SF0 + 5, SF0 + 6, SF0 + 7])
                        c.op("dve", lambda e: e.tensor_copy(out=ff(so)[:, 0:256], in_=PS[b][:, 0:256]), reads=Pk(b), writes=Fk(so))
                        dst = (mk_p if is_k else mv_p)[l, mt * 128:(mt + 1) * 128, cs * 256:(cs + 1) * 256]
                        c.dma("sp", "out", lambda e: e.dma_start(out=dst, in_=ff(so)[:, 0:256]), reads=Fk(so))
                        if not is_k:
                            c.op("act", lambda e: e.copy(out=mvv[:, mt, cs * 256:(cs + 1) * 256], in_=PS[b][:, 0:256]), reads=Pk(b),
                                 writes=Bk(MEM0 + l * 4 + 2, MEM0 + l * 4 + 3))

        c.mark(6)
        c.dma("sp", "pre", lambda e: e.dma_start(out=sTok, in_=xs), writes=sTokK)
        for ch in range(16):
            tr(PS[2][:, ch * 16:(ch + 1) * 16], sTok[:, ch * 128:(ch + 1) * 128], ident[0:16, 0:16], sTokK + ["ident"], ("P", 2), inc=(ch == 15))
        c.op("act", lambda e: e.copy(out=hTs[:].rearrange("p k b -> p (k b)"), in_=PS[2][:, 0:256]), reads=Pk(2), writes=S.hk)
        c.op("dve", lambda e: e.tensor_copy(out=xTs[:].rearrange("p k b -> p (k b)"), in_=PS[2][:, 0:256]), reads=Pk(2), writes=S.xk)

        def load_tile(ti):
            for sub in range(4):
                stg = ffr(SF0 + 1, 4)
                r0 = ti * T + sub * 128
                c.dma("sp", "xin", lambda e: e.dma_start(out=stg, in_=xp[r0:r0 + 128, :]), writes=Fk(range(SF0 + 1, SF0 + 5)))
                for cg in range(4):
                    b = nxt("pre_t", [0, 1])
                    for q in range(4):
                        ch = cg * 4 + q
                        tr(PS[b][:, q * 128:(q + 1) * 128], stg[:, ch * 128:(ch + 1) * 128], ident[:], Fk(range(SF0 + 1, SF0 + 5)) + ["ident"], ("P", b), inc=(q == 3))
                    src = PS[b][:].rearrange("p (q t) -> p q t", q=4)
                    c.op("act", lambda e: e.copy(out=FP[:, H0 + cg * 4:H0 + cg * 4 + 4, sub * 128:(sub + 1) * 128], in_=src),
                         reads=Pk(b), writes=P.hk[cg * 4:cg * 4 + 4])
                    c.op("dve", lambda e: e.tensor_copy(out=BFP[:, X0 + cg * 4:X0 + cg * 4 + 4, sub * 128:(sub + 1) * 128], in_=src),
                         reads=Pk(b), writes=P.xk[cg * 4:cg * 4 + 4])

        def store_tile(ti):
            for sub in range(4):
                stg = ffr(SF0 + 1, 4)
                for cg in range(4):
                    b = nxt("pre_t", [0, 1])
                    for q in range(4):
                        ch = cg * 4 + q
                        tr(PS[b][:, q * 128:(q + 1) * 128], P.hT[ch][:, sub * 128:(sub + 1) * 128], ident[:], [P.hk[ch], "ident"], ("P", b), inc=(q == 3))
                    c.op("act", lambda e: e.copy(out=stg[:, cg * 512:(cg + 1) * 512], in_=PS[b][:]), reads=Pk(b), writes=Fk(range(SF0 + 1, SF0 + 5)))
                r0 = ti * T + sub * 128
                c.dma("sp", "out", lambda e: e.dma_start(out=y_p[r0:r0 + 128, :], in_=stg), reads=Fk(range(SF0 + 1, SF0 + 5)))

        def store_sample():
            for cg in range(4):
                for q in range(4):
                    ch = cg * 4 + q
                    tr(PS[cg][0:16, q * 128:(q + 1) * 128], S.hT[ch], ident[:], [S.hk[ch], "ident"], ("P", cg), inc=(q == 3))
                c.op("act", lambda e: e.copy(out=sTok[:, cg * 512:(cg + 1) * 512], in_=PS[cg][0:16, :]), reads=Pk(cg), writes=sTokK)
            c.dma("sp", "out", lambda e: e.dma_start(out=y_s, in_=sTok), reads=sTokK)

        def mix_layer(ti, streams):
            uS, gS, vS = SB0, SB0 + 8, SB0 + 16
            cS = SF0
            has_s = len(streams) > 1

            def sink_u(st, ci, ps, pk):
                if st is P:
                    c.op("act", lambda e: e.copy(out=bf(uS + ci), in_=ps), reads=[pk], writes=Bk(uS + ci))
            def sink_gb(st, ci, ps, pk):
                if st is P:
                    c.op("act", lambda e: e.copy(out=bf(gS + ci), in_=ps), reads=[pk], writes=Bk(gS + ci))
                else:
                    c.op("act", lambda e: e.copy(out=sA[:, 8 + ci, :], in_=ps), reads=[pk], writes=[("sA", 8 + ci)])
            def sink_gc(st, ci, ps, pk):
                if st is P:
                    c.op("act", lambda e: e.copy(out=ff(cS + ci), in_=ps), reads=[pk], writes=Fk(cS + ci))
                else:
                    c.op("act", lambda e: e.copy(out=sF[:, ci, :], in_=ps), reads=[pk], writes=[("sF", ci)])
            def sink_hh(st, ci, ps, pk):
                if st is P:
                    c.op("dve", lambda e: e.tensor_tensor(out=ff(cS + ci), in0=ff(cS + ci), in1=ps, op=ALU.mult), reads=[pk] + Fk(cS + ci), writes=Fk(cS + ci))
                else:
                    c.op("dve", lambda e: e.tensor_tensor(out=sF[:, ci, :], in0=sF[:, ci, :], in1=ps, op=ALU.mult), reads=[pk, ("sF", ci)], writes=[("sF", ci)])
            proj_fm([P], w_in_mix, 0, 8, sink_u)
            vn = BFP[:, vS:vS + 8, :].rearrange("p (s h) c -> p s (h c)", s=4)
            for cs in range(4):
                view, wk, _ = wload(w_in_mix, 0, 16, 1024 + cs * 256, 256)

                def sink_v(sub, ps, pk, cs=cs):
                    x3 = ps.rearrange("p (g d) -> p g d", g=2)
                    s1 = small[:, 8:10]; s2 = small[:, 10:12]; t2 = small[:, 12:14]
                    sq = ff(SF0 + 8)[:, 0:256]
                    c.op("dve", lambda e: e.tensor_reduce(out=s1, in_=x3, axis=AX.X, op=ALU.add), reads=[pk], writes=["small"])
                    c.op("act", lambda e: e.activation(out=sq, in_=ps, func=AF.Square), reads=[pk], writes=Fk(SF0 + 8))
                    c.op("dve", lambda e: e.tensor_reduce(out=s2, in_=sq.rearrange("p (g d) -> p g d", g=2), axis=AX.X, op=ALU.add), reads=Fk(SF0 + 8), writes=["small"])
                    c.op("dve", lambda e: e.tensor_scalar(out=s1, in0=s1, scalar1=1.0 / 128, scalar2=None, op0=ALU.mult), reads=["small"], writes=["small"])
                    c.op("dve", lambda e: e.tensor_tensor(out=t2, in0=s1, in1=s1, op=ALU.mult), reads=["small"], writes=["small"])
                    c.op("dve", lambda e: e.scalar_tensor_tensor(out=s2, in0=s2, scalar=1.0 / 128, in1=t2, op0=ALU.mult, op1=ALU.subtract), reads=["small"], writes=["small"])
                    c.op("dve", lambda e: e.tensor_scalar(out=s2, in0=s2, scalar1=1e-5, scalar2=None, op0=ALU.add), reads=["small"], writes=["small"])
                    c.op("act", lambda e: e.activation(out=s2, in_=s2, func=AF.Sqrt), reads=["small"], writes=["small"])
                    c.op("dve", lambda e: e.reciprocal(out=s2, in_=s2), reads=["small"], writes=["small"])
                    sq3 = sq.rearrange("p (g d) -> p g d", g=2)
                    c.op("dve", lambda e: e.tensor_tensor(out=sq3, in0=x3, in1=s1.unsqueeze(2).to_broadcast([128, 2, 128]), op=ALU.subtract),
                         reads=[pk, "small"], writes=Fk(SF0 + 8))
                    c.op("dve", lambda e: e.tensor_tensor(out=vn[:, sub, cs * 256:(cs + 1) * 256].rearrange("p (g d) -> p g d", g=2), in0=sq3,
                                                          in1=s2.unsqueeze(2).to_broadcast([128, 2, 128]), op=ALU.mult),
                         reads=Fk(SF0 + 8) + ["small"], writes=Bk(range(vS, vS + 8)))
                proj_tm(P, None, 0, 256, 4, sink_v, view=view, wk=wk)
            for sub in range(4):
                for g in range(8):
                    if g % 4 == 0:
                        b = nxt("sg", [0, 1, 2, 3])
                    c.op("pe", lambda e: e.matmul(PS[b][:, (g % 4) * 128:(g % 4 + 1) * 128], lhsT=vn[:, sub, g * 128:(g + 1) * 128], rhs=swT[:, g, :], start=True, stop=True),
                         reads=Bk(range(vS, vS + 8)) + ["swT"], writes=Pk(b), inc=(g % 4 == 3))
                    if g % 4 == 3:
                        for gg in range(g - 3, g + 1):
                            tmp = ff(SF0 + 9)[:, (gg % 4) * 128:(gg % 4 + 1) * 128]
                            c.op("dve", lambda e: e.scalar_tensor_tensor(out=tmp, in0=PS[b][:, (gg % 4) * 128:(gg % 4 + 1) * 128], scalar=sguT[:, gg:gg + 1],
                                                                         in1=sgE[:, gg, :], op0=ALU.mult, op1=ALU.add), reads=Pk(b) + ["sguT", "sgE"], writes=Fk(SF0 + 9))
                            ua = bf(uS + gg)[:, sub * 128:(sub + 1) * 128]
                            c.op("dve", lambda e: e.tensor_tensor(out=ua, in0=ua, in1=tmp, op=ALU.mult), reads=Fk(SF0 + 9) + Bk(uS + gg), writes=Bk(uS + gg))
            proj_fm(streams, w_in_mix, 2048, 8, sink_gb)
            proj_fm(streams, w_in_mix, 3072, 8, sink_gc)
            proj_fm(streams, w_in_mix, 4096, 8, sink_hh)
            for ci in range(8):
                cc = ff(cS + ci)
                o = ff(SF0 + 8)
                t0c = convT[:, 0 * 8 + ci:0 * 8 + ci + 1]; t1c = convT[:, 8 + ci:8 + ci + 1]; t2c = convT[:, 16 + ci:16 + ci + 1]
                rk = Fk(cS + ci) + ["convT"]
                c.op("dve", lambda e: e.tensor_scalar(out=o, in0=cc, scalar1=t2c, scalar2=None, op0=ALU.mult), reads=rk, writes=Fk(SF0 + 8))
                c.op("dve", lambda e: e.scalar_tensor_tensor(out=o[:, 1:T], in0=cc[:, 0:T - 1], scalar=t1c, in1=o[:, 1:T], op0=ALU.mult, op1=ALU.add), reads=rk + Fk(SF0 + 8), writes=Fk(SF0 + 8))
                c.op("dve", lambda e: e.scalar_tensor_tensor(out=o[:, 2:T], in0=cc[:, 0:T - 2], scalar=t0c, in1=o[:, 2:T], op0=ALU.mult, op1=ALU.add), reads=rk + Fk(SF0 + 8), writes=Fk(SF0 + 8))
                c.op("dve", lambda e: e.scalar_tensor_tensor(out=o[:, 0:1], in0=halo[:, ci, 1:2], scalar=t1c, in1=o[:, 0:1], op0=ALU.mult, op1=ALU.add), reads=rk + Fk(SF0 + 8) + ["halo"], writes=Fk(SF0 + 8))
                c.op("dve", lambda e: e.scalar_tensor_tensor(out=o[:, 0:2], in0=halo[:, ci, 0:2], scalar=t0c, in1=o[:, 0:2], op0=ALU.mult, op1=ALU.add), reads=rk + Fk(SF0 + 8) + ["halo"], writes=Fk(SF0 + 8))
                c.op("dve", lambda e: e.tensor_copy(out=halo[:, ci, :], in_=cc[:, T - 2:T]), reads=rk + ["halo"], writes=["halo"])
                c.op("dve", lambda e: e.tensor_tensor(out=bf(gS + ci), in0=bf(gS + ci), in1=o, op=ALU.mult), reads=Fk(SF0 + 8) + Bk(gS + ci), writes=Bk(gS + ci))
            if ti == NT - 1:
                for j in range(2):
                    c.dma("sp", "out", lambda e: e.dma_start(out=cv_p[j, :].rearrange("(k p) -> p k", p=128), in_=halo[:, :, j]), reads=["halo"])
            in_aps = {"P": ([bf(uS + i) for i in range(16)], Bk(range(uS, uS + 16)))}
            if has_s:
                mix_sample()
                in_aps["S"] = ([sA[:, i, :] for i in range(16)], [("sA", i) for i in range(16)])
            wo = {}

            def slab(m):
                if m % 2 == 0:
                    wo["v"] = wload(w_out_mix, 0, 16, m * 128, 256)
                v, k, _ = wo["v"]
                return v, k, (m % 2) * 128
            out_linear(streams, 1, C_ONE, slab, in_aps, 16)

        def mix_sample():
            for (c0, dst, dk) in [(0, tA, tAk), (1024, tB, tBk)]:
                for cs in range(4):
                    view, wk, _ = wload(w_in_mix, 0, 16, c0 + cs * 256, 256)

                    def sink_t(sub, ps, pk, cs=cs, dst=dst, dk=dk):
                        c.op("act", lambda e: e.copy(out=dst[:, cs * 256:(cs + 1) * 256], in_=ps), reads=[pk], writes=dk)
                    proj_tm(S, None, 0, 256, 1, sink_t, view=view, wk=wk)
            gbc = ffr(SF0 + 8, 2)[0:16, :]; gbk = Fk(SF0 + 8, SF0 + 9)
            bbc = tB[:, 1024:2048]
            vv = tB[:, 0:1024]
            c.dma("sp", "smp", lambda e: e.dma_start(out=gbc, in_=sgu_g.rearrange("g d -> (g d)").unsqueeze(0).to_broadcast([16, 1024])), writes=gbk)
            c.dma("sp", "smp", lambda e: e.dma_start(out=bbc, in_=sgu_bb.rearrange("g d -> (g d)").unsqueeze(0).to_broadcast([16, 1024])), writes=tBk)
            v3 = vv.rearrange("p (g d) -> p g d", g=8)
            s1 = small[0:16, 16:24]; s2 = small[0:16, 24:32]; t2 = small[0:16, 32:40]
            sq = tA[:, 1024:2048]
            sq3 = sq.rearrange("p (g d) -> p g d", g=8)
            c.op("dve", lambda e: e.tensor_reduce(out=s1, in_=v3, axis=AX.X, op=ALU.add), reads=tBk, writes=["small"])
            c.op("act", lambda e: e.activation(out=sq, in_=vv, func=AF.Square), reads=tBk, writes=tAk)
            c.op("dve", lambda e: e.tensor_reduce(out=s2, in_=sq3, axis=AX.X, op=ALU.add), reads=tAk, writes=["small"])
            c.op("dve", lambda e: e.tensor_scalar(out=s1, in0=s1, scalar1=1.0 / 128, scalar2=None, op0=ALU.mult), reads=["small"], writes=["small"])
            c.op("dve", lambda e: e.tensor_tensor(out=t2, in0=s1, in1=s1, op=ALU.mult), reads=["small"], writes=["small"])
            c.op("dve", lambda e: e.scalar_tensor_tensor(out=s2, in0=s2, scalar=1.0 / 128, in1=t2, op0=ALU.mult, op1=ALU.subtract), reads=["small"], writes=["small"])
            c.op("dve", lambda e: e.tensor_scalar(out=s2, in0=s2, scalar1=1e-5, scalar2=None, op0=ALU.add), reads=["small"], writes=["small"])
            c.op("act", lambda e: e.activation(out=s2, in_=s2, func=AF.Sqrt), reads=["small"], writes=["small"])
            c.op("dve", lambda e: e.reciprocal(out=s2, in_=s2), reads=["small"], writes=["small"])
            c.op("dve", lambda e: e.tensor_tensor(out=v3, in0=v3, in1=s1.unsqueeze(2).to_broadcast([16, 8, 128]), op=ALU.subtract), reads=tBk + ["small"], writes=tBk)
            c.op("dve", lambda e: e.tensor_tensor(out=v3, in0=v3, in1=s2.unsqueeze(2).to_broadcast([16, 8, 128]), op=ALU.mult), reads=tBk + ["small"], writes=tBk)
            c.op("dve", lambda e: e.tensor_tensor(out=vv, in0=vv, in1=gbc, op=ALU.mult), reads=tBk + gbk, writes=tBk)
            c.op("dve", lambda e: e.tensor_tensor(out=vv, in0=vv, in1=bbc, op=ALU.add), reads=tBk, writes=tBk)
            c.dma("sp", "out", lambda e: e.dma_start(out=sv_s, in_=vv), reads=tBk)
            c.op("dve", lambda e: e.tensor_tensor(out=sq3, in0=v3, in1=w00[:].unsqueeze(2).to_broadcast([16, 8, 128]), op=ALU.mult), reads=tBk + ["w00"], writes=tAk)
            c.op("dve", lambda e: e.tensor_tensor(out=sq3, in0=sq3, in1=b00[:].unsqueeze(2).to_broadcast([16, 8, 128]), op=ALU.add), reads=tAk + ["b00"], writes=tAk)
            c.op("dve", lambda e: e.tensor_tensor(out=tA[:, 0:1024], in0=tA[:, 0:1024], in1=sq, op=ALU.mult), reads=tAk, writes=tAk)
            for ch in range(8):
                tr(PS[5][:, ch * 16:(ch + 1) * 16], tA[:, ch * 128:(ch + 1) * 128], ident[0:16, 0:16], tAk + ["ident"], ("P", 5), inc=(ch == 7))
            c.op("act", lambda e: e.copy(out=sA[:, 0:8, :].rearrange("p k b -> p (k b)"), in_=PS[5][:, 0:128]), reads=Pk(5), writes=[("sA", i) for i in range(8)])
            stg = ffr(SF0 + 8, 2)[0:32, :]
            c.dma("sp", "smp", lambda e: e.dma_start(out=stg, in_=stc), writes=gbk)
            for ch in range(8):
                tr(PS[4][:, ch * 32:(ch + 1) * 32], stg[:, ch * 128:(ch + 1) * 128], ident[0:32, 0:32], gbk + ["ident"], ("P", 4), inc=(ch == 7))
            c.op("act", lambda e: e.copy(out=stT[:].rearrange("p k x -> p (k x)"), in_=PS[4][:, 0:256]), reads=Pk(4), writes=["stT"])
            st4 = stT[:].rearrange("p k (b j) -> p k b j", j=2)
            cs3 = sF[:, 0:8, :]
            o3 = sF[:, 8:16, :]
            t3 = sF[:, 16:24, :]
            sfk = [("sF", i) for i in range(16)]
            sfk2 = [("sF", i) for i in range(16, 24)]
            c.op("dve", lambda e: e.tensor_tensor(out=o3, in0=cs3, in1=convT[:, 16:24].unsqueeze(2).to_broadcast([128, 8, NS]), op=ALU.mult), reads=sfk + ["convT"], writes=sfk)
            c.op("dve", lambda e: e.tensor_tensor(out=t3, in0=st4[:, :, :, 1], in1=convT[:, 8:16].unsqueeze(2).to_broadcast([128, 8, NS]), op=ALU.mult),
                 reads=["stT", "convT"], writes=sfk2)
            c.op("dve", lambda e: e.tensor_tensor(out=o3, in0=o3, in1=t3, op=ALU.add), reads=sfk + sfk2, writes=sfk)
            c.op("dve", lambda e: e.tensor_tensor(out=t3, in0=st4[:, :, :, 0], in1=convT[:, 0:8].unsqueeze(2).to_broadcast([128, 8, NS]), op=ALU.mult),
                 reads=["stT", "convT"], writes=sfk2)
            c.op("dve", lambda e: e.tensor_tensor(out=o3, in0=o3, in1=t3, op=ALU.add), reads=sfk + sfk2, writes=sfk)
            c.op("dve", lambda e: e.tensor_tensor(out=sA[:, 8:16, :], in0=sA[:, 8:16, :], in1=o3, op=ALU.mult), reads=sfk + [("sA", i) for i in range(8, 16)],
                 writes=[("sA", i) for i in range(8, 16)])
            for ch in range(8):
                pb_ = 5 if ch < 4 else 6
                tr(PS[pb_][0:16, (ch % 4) * 128:(ch % 4 + 1) * 128], sF[:, ch, :], ident[:], sfk + ["ident"], ("P", pb_), inc=(ch % 4 == 3))
            c.op("act", lambda e: e.copy(out=tA[:, 1024:1536], in_=PS[5][0:16, :]), reads=Pk(5), writes=tAk)
            c.op("act", lambda e: e.copy(out=tA[:, 1536:2048], in_=PS[6][0:16, :]), reads=Pk(6), writes=tAk)
            c.dma("sp", "out", lambda e: e.dma_start(out=cv_s[:, 1, :], in_=tA[:, 1024:2048]), reads=tAk)
            c.dma("sp", "out", lambda e: e.dma_start(out=cv_s[:, 0, :], in_=stc.rearrange("(b j) f -> b j f", j=2)[:, 1, :]))

        def mem_layer(ti, streams, l, li):
            qS = SB0
            has_s = len(streams) > 1

            def sink_q(st, ci, ps, pk):
                if st is P:
                    c.op("act", lambda e: e.copy(out=bf(qS + ci), in_=ps), reads=[pk], writes=Bk(qS + ci))
                else:
                    c.op("act", lambda e: e.copy(out=sA[:, ci, :], in_=ps), reads=[pk], writes=[("sA", ci)])
            proj_fm(streams, w_mem_q[l], 0, 4, sink_q)
            mkT = bfr(MEM0 + l * 4, 2).rearrange("p (h m) -> p h m", h=4)
            mvv = BFP[:, MEM0 + l * 4 + 2:MEM0 + l * 4 + 4, :]
            mkk = Bk(MEM0 + l * 4, MEM0 + l * 4 + 1); mvk = Bk(MEM0 + l * 4 + 2, MEM0 + l * 4 + 3)
            for h in range(4):
                pts = []
                for mt in range(2):
                    b = nxt("ms", [0, 1])
                    c.op("pe", lambda e: e.matmul(PS[b][:], lhsT=mkT[:, h, mt * 128:(mt + 1) * 128], rhs=bf(qS + h), start=True, stop=True),
                         reads=mkk + Bk(qS + h), writes=Pk(b))
                    ps_ = SB0 + 4 + (h % 2) * 2 + mt
                    c.op("act", lambda e: e.activation(out=bf(ps_), in_=PS[b][:], func=AF.Exp, scale=SCALE), reads=Pk(b), writes=Bk(ps_))
                    pts.append(ps_)
                ob, db = nxt("mo", [(2, 3), (4, 5)])
                for mt in range(2):
                    c.op("pe", lambda e: e.matmul(PS[ob][:], lhsT=mvv[:, mt, h * 128:(h + 1) * 128], rhs=bf(pts[mt]), start=(mt == 0), stop=(mt == 1)),
                         reads=mvk + Bk(pts[mt]), writes=Pk(ob), inc=False)
                    c.op("pe", lambda e: e.matmul(PS[db][:], lhsT=ones_b[:], rhs=bf(pts[mt]), start=(mt == 0), stop=(mt == 1)),
                         reads=Bk(pts[mt]) + ["ones_b"], writes=Pk(db), inc=(mt == 1))
                rd = ff(SF0 + 8 + h % 2)
                c.op("dve", lambda e: e.reciprocal(out=rd, in_=PS[db][:]), reads=Pk(db), writes=Fk(SF0 + 8 + h % 2))
                c.op("dve", lambda e: e.tensor_tensor(out=bf(qS + h), in0=PS[ob][:], in1=rd, op=ALU.mult), reads=Pk(ob) + Fk(SF0 + 8 + h % 2), writes=Bk(qS + h))
            in_aps = {"P": ([bf(qS + i) for i in range(4)], Bk(range(qS, qS + 4)))}
            if has_s:
                mem_sample(l)
                in_aps["S"] = ([sA[:, i, :] for i in range(4)], [("sA", i) for i in range(4)])
            wo = {}

            def slab(m):
                if m % 8 == 0:
                    wo["v"] = wload(w_mem_o[l], 0, 4, m * 128, 1024)
                v, k, _ = wo["v"]
                return v, k, (m % 8) * 128
            out_linear(streams, li, C_ONE, slab, in_aps, 4)

        def mem_sample(l):
            kS, vS_, tS, pS_ = SB0 + 8, SB0 + 10, SB0 + 12, SB0 + 14
            for b_ in range(NS):
                kb = BFP[:, kS:kS + 2, :]; vb = BFP[:, vS_:vS_ + 2, :]
                c.dma("pool", "smk", lambda e: e.dma_start(out=kb, in_=cmk[l, b_].rearrange("(t p) f -> p t f", p=128)), writes=Bk(kS, kS + 1))
                c.dma("pool", "smv", lambda e: e.dma_start(out=vb, in_=cmv[l, b_].rearrange("(t p) f -> p t f", p=128)), writes=Bk(vS_, vS_ + 1))
                kT = bfr(tS, 2).rearrange("p (h m) -> p h m", h=4)
                pb = PS[0][:].bitcast(BF16)
                for h in range(4):
                    for mt in range(2):
                        tr(pb[:, (h * 2 + mt) * 128:(h * 2 + mt + 1) * 128], kb[:, mt, h * 128:(h + 1) * 128], idb[:], Bk(kS, kS + 1) + ["idb"], ("P", 0), inc=(h == 3 and mt == 1))
                c.op("act", lambda e: e.copy(out=bfr(tS, 2), in_=pb[:, 0:1024]), reads=Pk(0), writes=Bk(tS, tS + 1))
                for h in range(4):
                    for mt in range(2):
                        c.op("pe", lambda e: e.matmul(PS[1][:, h * 2 + mt:h * 2 + mt + 1], lhsT=kT[:, h, mt * 128:(mt + 1) * 128], rhs=sA[:, h, b_:b_ + 1], start=True, stop=True),
                             reads=Bk(tS, tS + 1) + [("sA", h)], writes=Pk(1), inc=(h == 3 and mt == 1))
                pT = bf(pS_)[:, 0:8]
                c.op("act", lambda e: e.activation(out=pT, in_=PS[1][:, 0:8], func=AF.Exp, scale=SCALE), reads=Pk(1), writes=Bk(pS_))
                for h in range(4):
                    for mt in range(2):
                        c.op("pe", lambda e: e.matmul(PS[2][:, h:h + 1], lhsT=vb[:, mt, h * 128:(h + 1) * 128], rhs=pT[:, h * 2 + mt:h * 2 + mt + 1], start=(mt == 0), stop=(mt == 1)),
                             reads=Bk(vS_, vS_ + 1, pS_), writes=Pk(2), inc=False)
                    for mt in range(2):
                        c.op("pe", lambda e: e.matmul(PS[2][:, 4 + h:5 + h], lhsT=ones_b[:], rhs=pT[:, h * 2 + mt:h * 2 + mt + 1], start=(mt == 0), stop=(mt == 1)),
                             reads=Bk(pS_) + ["ones_b"], writes=Pk(2), inc=(h == 3 and mt == 1))
                c.op("dve", lambda e: e.reciprocal(out=small[:, 40:44], in_=PS[2][:, 4:8]), reads=Pk(2), writes=["small"])
                c.op("dve", lambda e: e.tensor_tensor(out=sBt[:, 0:4, b_], in0=PS[2][:, 0:4], in1=small[:, 40:44], op=ALU.mult), reads=Pk(2) + ["small"], writes=[("sBt", 0)])
            c.op("dve", lambda e: e.tensor_copy(out=sA[:, 0:4, :], in_=sBt[:, 0:4, :]), reads=[("sBt", 0)], writes=[("sA", i) for i in range(4)])

        def dsa_layer(ti, streams):
            has_s = len(streams) > 1
            qS = SB0
            qiS = SB0 + 16
            mS = SB0 + 24
            kiT = bfr(KI0, 4)

            def sink_q(st, ci, ps, pk):
                if st is P:
                    c.op("act", lambda e: e.copy(out=bf(qS + ci), in_=ps), reads=[pk], writes=Bk(qS + ci))
                else:
                    c.op("act", lambda e: e.copy(out=sA[:, ci, :], in_=ps), reads=[pk], writes=[("sA", ci)])

            def sink_k(st, ci, ps, pk):
                if st is P:
                    c.op("act", lambda e: e.copy(out=bf(KT0 + ci * 4 + ti), in_=ps), reads=[pk], writes=Bk(KT0 + ci * 4 + ti))

            def sink_qi(st, ci, ps, pk):
                if st is P:
                    c.op("act", lambda e: e.copy(out=bf(qiS + ci), in_=ps), reads=[pk], writes=Bk(qiS + ci))
            proj_fm(streams, w_in_attn, 0, 16, sink_q)
            for cs in range(2):
                view, wk, _ = wload(w_in_attn, 0, 16, 2048 + cs * 256, 256)
                for jj in range(2):
                    b = nxt("pfP", [0, 1, 2])
                    mm(PS[b][:], [(view[:, kc, jj * 128:(jj + 1) * 128], P.xT[kc]) for kc in range(16)], [wk] + P.xk, ("P", b))
                    sink_k(P, cs * 2 + jj, PS[b][:], ("P", b))

                def sink_kt(sub, ps, pk, cs=cs):
                    so = nxt("ost", [SF0 + 8, SF0 + 9])
                    c.op("dve", lambda e: e.tensor_copy(out=ff(so)[:, 0:256], in_=ps), reads=[pk], writes=Fk(so))
                    r0 = ti * T + sub * 128
                    c.dma("sp", "out", lambda e: e.dma_start(out=ak_p[r0:r0 + 128, cs * 256:(cs + 1) * 256], in_=ff(so)[:, 0:256]), reads=Fk(so))
                proj_tm(P, None, 0, 256, 4, sink_kt, view=view, wk=wk)
            for cs in range(2):
                view, wk, _ = wload(w_in_attn, 0, 16, 2560 + cs * 256, 256)

                def sink_vt(sub, ps, pk, cs=cs):
                    so = nxt("ost", [SF0 + 8, SF0 + 9])
                    c.op("dve", lambda e: e.tensor_copy(out=ff(so)[:, 0:256], in_=ps), reads=[pk], writes=Fk(so))
                    c.op("act", lambda e: e.copy(out=bf(V0 + ti * 4 + sub)[:, cs * 256:(cs + 1) * 256], in_=ps), reads=[pk], writes=Bk(V0 + ti * 4 + sub))
                    r0 = ti * T + sub * 128
                    c.dma("sp", "out", lambda e: e.dma_start(out=av_p[r0:r0 + 128, cs * 256:(cs + 1) * 256], in_=ff(so)[:, 0:256]), reads=Fk(so))
                proj_tm(P, None, 0, 256, 4, sink_vt, view=view, wk=wk)
            proj_fm([P], w_in_attn, 3072, 8, sink_qi)
            slot = nxt("w", list(range(NW)))
            view, wk, _ = wload(w_in_attn, 0, 16, 4096, 64, dst_col=0, slot=slot, width=160)
            wload(w_in_attn, 0, 16, 4096, 64, dst_col=64, slot=slot, width=160)
            wload(w_in_attn, 0, 16, 4160, 16, dst_col=128, slot=slot, width=160)
            b = nxt("pfP", [0, 1, 2])
            mm(PS[b][:], [(view[:, kc, 0:128], P.xT[kc]) for kc in range(16)], [wk] + P.xk, ("P", b))
            c.op("act", lambda e: e.copy(out=bf(KI0 + ti), in_=PS[b][:]), reads=Pk(b), writes=Bk(KI0 + ti))

            def sink_kiwi(sub, ps, pk):
                so = nxt("ost", [SF0 + 8, SF0 + 9])
                c.op("dve", lambda e: e.tensor_copy(out=ff(so)[:, 0:64], in_=ps[:, 0:64]), reads=[pk], writes=Fk(so))
                c.op("act", lambda e: e.copy(out=wi_sb[:, sub, :], in_=ps[:, 64:80]), reads=[pk], writes=["wi_sb"])
                r0 = ti * T + sub * 128
                c.dma("sp", "out", lambda e: e.dma_start(out=ik_p[r0:r0 + 128, :], in_=ff(so)[:, 0:64]), reads=Fk(so))
            proj_tm(P, None, 0, 80, 4, sink_kiwi, view=view, wk=wk, voff=64)
            if has_s:
                mm(PS[3][:, 0:NS], [(view[:, kc, 0:128], S.xT[kc]) for kc in range(16)], [wk] + S.xk, ("P", 3))
                c.op("act", lambda e: e.copy(out=kiTs_new[:], in_=PS[3][:, 0:NS]), reads=Pk(3), writes=["kiTs_new"])
                mm(PS[3][0:16, 32:32 + NS], [(view[:, kc, 128:144], S.xT[kc]) for kc in range(16)], [wk] + S.xk, ("P", 3))
                c.op("act", lambda e: e.copy(out=wiTs[:], in_=PS[3][0:16, 32:32 + NS]), reads=Pk(3), writes=["wiTs"])


            accS, wrkS = SF0, SF0 + 4
            mtS = SB0 + 28
            maskT = bfr(mtS, 4).rearrange("p (j t) -> p j t", j=16)
            for ql in range(4):
                qb = ti * 4 + ql
                nk = (qb + 1) * 128
                acc = ffr(accS, 4)[:, 0:nk]; wrk = ffr(wrkS, 4)[:, 0:nk]
                acck = Fk(range(accS, accS + 4)); wrkk = Fk(range(wrkS, wrkS + 4))
                nch = (nk + 511) // 512
                for h in range(16):
                    lo = (h % 2) * 64
                    lhs = bf(qiS + h // 2)[lo:lo + 64, ql * 128:(ql + 1) * 128]
                    for kc in range(nch):
                        w_ = min(512, nk - kc * 512)
                        b = nxt("is", [0, 1])
                        c.op("pe", lambda e: e.matmul(PS[b][:, 0:w_], lhsT=lhs, rhs=kiT[lo:lo + 64, kc * 512:kc * 512 + w_], start=True, stop=True),
                             reads=Bk(qiS + h // 2) + Bk(range(KI0, KI0 + 4)), writes=Pk(b))
                        rt = nxt("rl", [SF0 + 8, SF0 + 9])
                        c.op("act", lambda e: e.activation(out=ff(rt)[:, 0:w_], in_=PS[b][:, 0:w_], func=AF.Relu), reads=Pk(b), writes=Fk(rt))
                        a_ = acc[:, kc * 512:kc * 512 + w_]
                        wcol = wi_sb[:, ql, h:h + 1]
                        if h == 0:
                            c.op("dve", lambda e: e.tensor_scalar(out=a_, in0=ff(rt)[:, 0:w_], scalar1=wcol, scalar2=None, op0=ALU.mult), reads=Fk(rt) + ["wi_sb"], writes=acck)
                        else:
                            c.op("dve", lambda e: e.scalar_tensor_tensor(out=a_, in0=ff(rt)[:, 0:w_], scalar=wcol, in1=a_, op0=ALU.mult, op1=ALU.add), reads=Fk(rt) + ["wi_sb"] + acck, writes=acck)
                c.op("dve", lambda e: e.tensor_tensor(out=acc[:, nk - 128:nk], in0=acc[:, nk - 128:nk], in1=cmask[:], op=ALU.add), reads=acck + ["cmask"], writes=acck)
                maskv = bfr(mS, 4)[:, 0:nk]
                if qb >= 2:
                    for r in range(32):
                        src = acc if r == 0 else wrk
                        c.op("dve", lambda e: e.max(out=m8[:], in_=src), reads=(acck if r == 0 else wrkk), writes=["m8"])
                        if r < 31:
                            c.op("dve", lambda e: e.match_replace(out=wrk, in_to_replace=m8[:], in_values=src, imm_value=NEG), reads=["m8"] + (acck if r == 0 else wrkk), writes=wrkk)
                    c.op("dve", lambda e: e.tensor_scalar(out=maskv, in0=acc, scalar1=m8[:, 7:8], scalar2=None, op0=ALU.is_ge), reads=acck + ["m8"], writes=Bk(range(mS, mS + 4)))
                else:
                    c.op("dve", lambda e: e.tensor_scalar(out=maskv, in0=acc, scalar1=-1.0e29, scalar2=None, op0=ALU.is_ge), reads=acck, writes=Bk(range(mS, mS + 4)))
                pbt = PS[2][:].bitcast(BF16)
                for j0 in range(0, qb + 1, 8):
                    jn = min(8, qb + 1 - j0)
                    for j in range(j0, j0 + jn):
                        tr(pbt[:, (j - j0) * 128:(j - j0 + 1) * 128], maskv[:, j * 128:(j + 1) * 128], idb[:], Bk(range(mS, mS + 4)) + ["idb"], ("P", 2), inc=(j == j0 + jn - 1))
                    c.op("act", lambda e: e.copy(out=maskT[:, j0:j0 + jn, :].rearrange("p j t -> p (j t)"), in_=pbt[:, 0:jn * 128]), reads=Pk(2), writes=Bk(range(mtS, mtS + 4)))
                for kv in range(4):
                    rhsq = BFP[:, qS + kv * 4:qS + kv * 4 + 4, ql * 128:(ql + 1) * 128]
                    qk = Bk(range(qS + kv * 4, qS + kv * 4 + 4))
                    ob, db = nxt("ao", [(5, 6), (7, 3)])
                    for j in range(qb + 1):
                        b = nxt("as", [0, 1, 4])
                        ktile = bf(KT0 + kv * 4 + j // 4)[:, (j % 4) * 128:(j % 4 + 1) * 128]
                        c.op("pe", lambda e: e.matmul(PS[b][:].rearrange("p (h t) -> p h t", h=4), lhsT=ktile, rhs=rhsq, start=True, stop=True),
                             reads=Bk(KT0 + kv * 4 + j // 4) + qk, writes=Pk(b))
                        ps_ = nxt("pt_", [SB0 + 32, SB0 + 33, SB0 + 34, SB0 + 35])
                        c.op("act", lambda e: e.activation(out=bf(ps_), in_=PS[b][:], func=AF.Exp, scale=SCALE), reads=Pk(b), writes=Bk(ps_))
                        p3 = bf(ps_).rearrange("p (h t) -> p h t", h=4)
                        c.op("pool", lambda e: e.tensor_tensor(out=p3, in0=p3, in1=maskT[:, j, :].unsqueeze(1).to_broadcast([128, 4, 128]), op=ALU.mult),
                             reads=Bk(ps_) + Bk(range(mtS, mtS + 4)), writes=Bk(ps_))
                        vt = bf(V0 + j)[:, kv * 128:(kv + 1) * 128]
                        c.op("pe", lambda e: e.matmul(PS[ob][:], lhsT=vt, rhs=bf(ps_), start=(j == 0), stop=(j == qb)), reads=Bk(V0 + j, ps_), writes=Pk(ob), inc=False)
                        c.op("pe", lambda e: e.matmul(PS[db][:], lhsT=ones_b[:], rhs=bf(ps_), start=(j == 0), stop=(j == qb)), reads=Bk(ps_) + ["ones_b"], writes=Pk(db), inc=(j == qb))
                    rd = nxt("ard", [SF0 + 8, SF0 + 9])
                    c.op("dve", lambda e: e.reciprocal(out=ff(rd), in_=PS[db][:]), reads=Pk(db), writes=Fk(rd))
                    c.op("dve", lambda e: e.tensor_tensor(out=rhsq, in0=PS[ob][:].rearrange("p (h t) -> p h t", h=4), in1=ff(rd).rearrange("p (h t) -> p h t", h=4), op=ALU.mult),
                         reads=Pk(ob) + Fk(rd), writes=qk)
            in_aps = {"P": ([bf(qS + i) for i in range(16)], Bk(range(qS, qS + 16)))}
            if has_s:
                dsa_sample()
                in_aps["S"] = ([sA[:, i, :] for i in range(16)], [("sA", i) for i in range(16)])
            wo = {}

            def slab(m):
                if m % 2 == 0:
                    wo["v"] = wload(w_out_attn, 0, 16, m * 128, 256)
                v, k, _ = wo["v"]
                return v, k, (m % 2) * 128
            out_linear(streams, 5, C_ONE, slab, in_aps, 16)

        def dsa_sample():
            for (c0, off) in [(2048, 0), (2560, 512)]:
                for cs in range(2):
                    view, wk, _ = wload(w_in_attn, 0, 16, c0 + cs * 256, 256)

                    def sink_t(sub, ps, pk, cs=cs, off=off):
                        c.op("act", lambda e: e.copy(out=tA[:, off + cs * 256:off + (cs + 1) * 256], in_=ps), reads=[pk], writes=tAk)
                    proj_tm(S, None, 0, 256, 1, sink_t, view=view, wk=wk)
            view, wk, _ = wload(w_in_attn, 0, 16, 4096, 64)

            def sink_ki(sub, ps, pk):
                c.op("act", lambda e: e.copy(out=tA[:, 1024:1088], in_=ps), reads=[pk], writes=tAk)
            proj_tm(S, None, 0, 64, 1, sink_ki, view=view, wk=wk)
            c.dma("sp", "outs", lambda e: e.dma_start(out=ak_s, in_=tA[:, 0:512]), reads=tAk, writes=["d_ak_s"])
            c.dma("sp", "outs", lambda e: e.dma_start(out=av_s, in_=tA[:, 512:1024]), reads=tAk, writes=["d_av_s"])
            c.dma("sp", "outs", lambda e: e.dma_start(out=ik_s, in_=tA[:, 1024:1088]), reads=tAk)
            for cs in range(4):
                view, wk, _ = wload(w_in_attn, 0, 16, 3072 + cs * 256, 256)

                def sink_q(sub, ps, pk, cs=cs):
                    c.op("act", lambda e: e.copy(out=tB[:, cs * 256:(cs + 1) * 256], in_=ps), reads=[pk], writes=tBk)
                proj_tm(S, None, 0, 256, 1, sink_q, view=view, wk=wk)
            for h in range(16):
                tr(PS[0][0:64, h * 16:(h + 1) * 16], tB[:, h * 64:(h + 1) * 64], ident[0:16, 0:16], tBk + ["ident"], ("P", 0), inc=(h == 15))
            c.op("act", lambda e: e.copy(out=qiTs[:].rearrange("p h b -> p (h b)"), in_=PS[0][0:64, 0:256]), reads=Pk(0), writes=["qiTs"])
            c.op("dve", lambda e: e.tensor_tensor(out=wsel[:], in0=wiTs[:].unsqueeze(2).to_broadcast([16, 16, 16]), in1=eye16[:].rearrange("p (a b) -> p a b", a=16), op=ALU.mult),
                 reads=["wiTs", "eye16"], writes=["wsel"])
            kiS = SB0 + 16
            kiTb = bfr(kiS, 5)[0:64, 0:SW]; kik = Bk(range(kiS, kiS + 5))
            c.op("dve", lambda e: e.memset(kiTb[:, 2048:SW], 0.0), writes=kik)
            scS, wkS = SF0, SF0 + 5
            sc = ffr(scS, 5)[0:16, 0:SW]; wrk = ffr(wkS, 5)[0:16, 0:SW]
            sck = Fk(range(scS, scS + 5)); wkk = Fk(range(wkS, wkS + 5))
            widths = [512, 512, 512, 512, 128]
            for b_ in range(NS):
                for pg in range(4):
                    for q in range(4):
                        p_ = pg * 4 + q
                        gs = nxt("kig", [wkS, wkS + 1, wkS + 2, wkS + 3])
                        c.dma("pool", f"kig{gs}", lambda e: e.indirect_dma_start(out=ff(gs)[:, 0:64], out_offset=None, in_=cik,
                                                                                in_offset=bass.IndirectOffsetOnAxis(ap=idxall[:, b_ * 16 + p_:b_ * 16 + p_ + 1], axis=0)),
                              reads=["idxall"], writes=Fk(gs))
                        tr(PS[1][0:64, q * 128:(q + 1) * 128], ff(gs)[:, 0:64], ident[:], Fk(gs) + ["ident"], ("P", 1), inc=(q == 3))
                    c.op("act", lambda e: e.copy(out=kiTb[:, pg * 512:(pg + 1) * 512], in_=PS[1][0:64, :]), reads=Pk(1), writes=kik)
                c.op("dve", lambda e: e.tensor_copy(out=kiTb[:, 2048:2049], in_=kiTs_new[0:64, b_:b_ + 1]), reads=["kiTs_new"], writes=kik)
                for kc in range(5):
                    w_ = widths[kc]
                    c.op("pe", lambda e: e.matmul(PS[2][0:16, 0:w_], lhsT=qiTs[:, :, b_], rhs=kiTb[:, kc * 512:kc * 512 + w_], start=True, stop=True),
                         reads=["qiTs"] + kik, writes=Pk(2))
                    rl = ff(wkS + 4)[0:16, 0:w_]
                    c.op("act", lambda e: e.activation(out=rl, in_=PS[2][0:16, 0:w_], func=AF.Relu), reads=Pk(2), writes=Fk(wkS + 4))
                    c.op("pe", lambda e: e.matmul(PS[3 + kc][0:16, 0:w_], lhsT=wsel[:, b_, :], rhs=rl, start=(b_ == 0), stop=(b_ == NS - 1)),
                         reads=Fk(wkS + 4) + ["wsel"], writes=Pk(3 + kc), inc=True)
            for kc in range(5):
                w_ = widths[kc]
                c.op("dve", lambda e: e.tensor_copy(out=sc[:, kc * 512:kc * 512 + w_], in_=PS[3 + kc][0:16, 0:w_]), reads=Pk(3 + kc), writes=sck)
            c.op("dve", lambda e: e.memset(sc[:, 2049:SW], NEG), reads=sck, writes=sck)
            for r in range(32):
                src = sc if r == 0 else wrk
                c.op("dve", lambda e: e.max(out=m8[0:16, :], in_=src), reads=(sck if r == 0 else wkk), writes=["m8"])
                if r < 31:
                    c.op("dve", lambda e: e.match_replace(out=wrk, in_to_replace=m8[0:16, :], in_values=src, imm_value=NEG), reads=["m8"] + (sck if r == 0 else wkk), writes=wkk)
            msS = SB0 + 21
            msk = BFP[0:16, msS:msS + 5, :].rearrange("p s c -> p (s c)")[:, 0:SW]
            mskk = Bk(range(msS, msS + 5))
            c.op("dve", lambda e: e.tensor_scalar(out=msk, in0=sc, scalar1=m8[0:16, 7:8], scalar2=None, op0=ALU.is_ge), reads=sck + ["m8"], writes=mskk)
            pbt = PS[0][:].bitcast(BF16)
            for p_ in range(17):
                tr(pbt[:, p_ * 16:(p_ + 1) * 16], msk[:, p_ * 128:(p_ + 1) * 128], idb[0:16, 0:16], mskk + ["idb"], ("P", 0), inc=(p_ == 16))
            c.op("act", lambda e: e.copy(out=maskTs[:].rearrange("p j b -> p (j b)"), in_=pbt[:, 0:17 * 16]), reads=Pk(0), writes=["maskTs"])
            for b_ in range(NS):
                for p_ in range(17):
                    kf = nxt("kpf", [SF0, SF0 + 1])
                    if p_ < 16:
                        c.dma("pool", f"kpg{kf}", lambda e: e.indirect_dma_start(out=ff(kf), out_offset=None, in_=cak,
                                                                                in_offset=bass.IndirectOffsetOnAxis(ap=idxall[:, b_ * 16 + p_:b_ * 16 + p_ + 1], axis=0)),
                              reads=["idxall"], writes=Fk(kf))
                    else:
                        c.op("dve", lambda e: e.memset(ff(kf), 0.0), writes=Fk(kf))
                        c.dma("sp", "knew", lambda e: e.dma_start(out=ff(kf)[0:1, :], in_=ak_s[b_:b_ + 1, :]), reads=["d_ak_s"], writes=Fk(kf))
                    kb = nxt("kpb", [SB0 + 26, SB0 + 27])
                    c.op("act", lambda e: e.copy(out=bf(kb), in_=ff(kf)), reads=Fk(kf), writes=Bk(kb))
                    pb1 = PS[1][:].bitcast(BF16)
                    for kv in range(4):
                        tr(pb1[:, kv * 128:(kv + 1) * 128], bf(kb)[:, kv * 128:(kv + 1) * 128], idb[:], Bk(kb) + ["idb"], ("P", 1), inc=(kv == 3))
                    kt = nxt("kpt", [SB0 + 28, SB0 + 29])
                    c.op("dve", lambda e: e.tensor_copy(out=bf(kt), in_=pb1[:, 0:512]), reads=Pk(1), writes=Bk(kt))
                    for kv in range(4):
                        c.op("pe", lambda e: e.matmul(PS[2][:, p_ * 16 + kv * 4:p_ * 16 + kv * 4 + 4], lhsT=bf(kt)[:, kv * 128:(kv + 1) * 128], rhs=sA[:, kv * 4:kv * 4 + 4, b_],
                                                      start=True, stop=True), reads=Bk(kt) + [("sA", i) for i in range(kv * 4, kv * 4 + 4)], writes=Pk(2), inc=(kv == 3))
                c.op("act", lambda e: e.activation(out=pTf[:].rearrange("p j h -> p (j h)"), in_=PS[2][:, 0:272], func=AF.Exp, scale=SCALE), reads=Pk(2), writes=["pTf"])
                c.op("dve", lambda e: e.tensor_tensor(out=pTs[:], in0=pTf[:], in1=maskTs[:, :, b_].unsqueeze(2).to_broadcast([128, 17, 16]), op=ALU.mult),
                     reads=["pTf", "maskTs"], writes=["pTs"])
                for p_ in range(17):
                    vf = nxt("vpf", [SF0 + 2, SF0 + 3])
                    if p_ < 16:
                        c.dma("pool", f"vpg{vf}", lambda e: e.indirect_dma_start(out=ff(vf), out_offset=None, in_=cav,
                                                                                in_offset=bass.IndirectOffsetOnAxis(ap=idxall[:, b_ * 16 + p_:b_ * 16 + p_ + 1], axis=0)),
                              reads=["idxall"], writes=Fk(vf))
                    else:
                        c.op("dve", lambda e: e.memset(ff(vf), 0.0), writes=Fk(vf))
                        c.dma("sp", "knew", lambda e: e.dma_start(out=ff(vf)[0:1, :], in_=av_s[b_:b_ + 1, :]), reads=["d_av_s"], writes=Fk(vf))
                    vb = nxt("vpb", [SB0 + 30, SB0 + 31])
                    c.op("act", lambda e: e.copy(out=bf(vb), in_=ff(vf)), reads=Fk(vf), writes=Bk(vb))
                    for kv in range(4):
                        c.op("pe", lambda e: e.matmul(PS[3][:, kv * 4:kv * 4 + 4], lhsT=bf(vb)[:, kv * 128:(kv + 1) * 128], rhs=pTs[:, p_, kv * 4:kv * 4 + 4],
                                                      start=(p_ == 0 and kv == 0), stop=(p_ == 16), skip_group_check=True), reads=Bk(vb) + ["pTs"], writes=Pk(3), inc=False)
                    c.op("pe", lambda e: e.matmul(PS[3][:, 16:32], lhsT=ones_b[:], rhs=pTs[:, p_, :], start=False, stop=(p_ == 16), skip_group_check=True),
                         reads=["pTs", "ones_b"], writes=Pk(3), inc=True)
                c.op("dve", lambda e: e.reciprocal(out=small[:, 44:60], in_=PS[3][:, 16:32]), reads=Pk(3), writes=["small"])
                c.op("dve", lambda e: e.tensor_tensor(out=sBt[:, :, b_], in0=PS[3][:, 0:16], in1=small[:, 44:60], op=ALU.mult), reads=Pk(3) + ["small"], writes=[("sBt", 0)])
            c.op("dve", lambda e: e.tensor_copy(out=sA[:], in_=sBt[:]), reads=[("sBt", 0)], writes=[("sA", i) for i in range(16)])

        import os
        K_NT = int(os.environ.get("K_NT", NT)); K_STAGE = int(os.environ.get("K_STAGE", 99)); K_NOS = int(os.environ.get("K_NOS", 0))
        for ti in range(K_NT):
            streams = [P, S] if (ti == 0 and not K_NOS) else [P]
            load_tile(ti)
            stages = [lambda: ffn(streams, 0, 0), lambda: mix_layer(ti, streams), lambda: mem_layer(ti, streams, 0, 2), lambda: ffn(streams, 1, 3),
                      lambda: ffn(streams, 2, 4), lambda: dsa_layer(ti, streams), lambda: mem_layer(ti, streams, 1, 6), lambda: ffn(streams, 3, 7)]
            for si, fn_ in enumerate(stages):
                if si < K_STAGE:
                    fn_()
            store_tile(ti)
            if ti == 0 and not K_NOS:
                store_sample()
        c.finish("sp")
        print("instructions emitted:", c.ninst)
    return nc


_CACHE = {}


def kernel(**inp):
    f32 = np.float32
    g = lambda k: np.asarray(inp[k])
    if "nc" not in _CACHE:
        _CACHE["nc"] = build_program()
    nc = _CACHE["nc"]
    shared = {
        "cak": g("cache_attn_k")[0].reshape(NPOOL * 128, 512), "cav": g("cache_attn_v")[0].reshape(NPOOL * 128, 512),
        "cik": g("cache_idx_k")[0].reshape(NPOOL * 128, 64),
        "w_in_mix": g("w_in_mix")[0], "sgu_w": g("sgu_w")[0], "sgu_b": g("sgu_b")[0].reshape(1, 1024),
        "sgu_g": g("sgu_ln_g")[0], "sgu_bb": g("sgu_ln_b")[0], "conv_w": g("conv_w")[0].reshape(24, 128),
        "w_out_mix": g("w_out_mix")[0], "w_in_attn": g("w_in_attn")[0], "w_out_attn": g("w_out_attn")[0],
        "w_mem_q": g("w_mem_q"), "w_mem_k": g("w_mem_k"), "w_mem_v": g("w_mem_v"), "w_mem_o": g("w_mem_o"),
        "w_gu": g("ffn_w_gu").reshape(4, D, 2 * DFF), "w_dn": g("ffn_w_down").reshape(4, DFF, D),
        "ln_g": g("ln_g").reshape(128, 128), "ln_b": g("ln_b").reshape(128, 128),
        "c_ident": np.eye(128, dtype=f32), "c_triu": np.triu(np.ones((128, 128), f32)),
        "c_cmask": np.where(np.arange(128)[None, :] <= np.arange(128)[:, None], 0.0, NEG).astype(f32),
        "c_eye16": np.tile(np.eye(16, dtype=f32).reshape(1, 256), (16, 1)),
    }
    shared = {k: np.ascontiguousarray(v) for k, v in shared.items()}
    in_maps = []
    for cid in range(NCORES):
        sl = slice(cid * NS, (cid + 1) * NS)
        m = dict(shared)
        m["xp"] = np.ascontiguousarray(g("x_prompt")[cid])
        m["xs"] = np.ascontiguousarray(g("x_sample")[sl, 0])
        m["stc"] = np.ascontiguousarray(g("state_conv")[0, sl].reshape(NS * 2, 1024))
        m["cmk"] = np.ascontiguousarray(g("cache_mem_k")[:, sl].reshape(2, NS, 256, 512))
        m["cmv"] = np.ascontiguousarray(g("cache_mem_v")[:, sl].reshape(2, NS, 256, 512))
        m["ptab"] = np.ascontiguousarray(g("page_table")[sl].reshape(1, NS * 16).astype(np.int32))
        m["memp"] = np.ascontiguousarray(g("mem_prompt")[cid])
        in_maps.append(m)
    res = run_bass_kernel_spmd(nc, in_maps, core_ids=list(range(NCORES)))
    R = res.results

    def cat(k, shape_per):
        return np.stack([np.asarray(R[i][k]).reshape(shape_per) for i in range(NCORES)], 0)
    y_prompt = cat("y_p", (SEQ, D))
    y_sample = np.concatenate([np.asarray(R[i]["y_s"]).reshape(NS, 1, D) for i in range(NCORES)], 0)
    ak_p = cat("ak_p", (SEQ, 4, 128))[None]
    av_p = cat("av_p", (SEQ, 4, 128))[None]
    ik_p = cat("ik_p", (SEQ, 64))[None]
    cv_p = cat("cv_p", (2, 1024))[None]
    mk_p = np.stack([np.asarray(R[i]["mk_p"]).reshape(2, 256, 4, 128) for i in range(NCORES)], 1)
    mv_p = np.stack([np.asarray(R[i]["mv_p"]).reshape(2, 256, 4, 128) for i in range(NCORES)], 1)
    ak_s = np.concatenate([np.asarray(R[i]["ak_s"]).reshape(NS, 1, 4, 128) for i in range(NCORES)], 0)[None]
    av_s = np.concatenate([np.asarray(R[i]["av_s"]).reshape(NS, 1, 4, 128) for i in range(NCORES)], 0)[None]
    ik_s = np.concatenate([np.asarray(R[i]["ik_s"]).reshape(NS, 1, 64) for i in range(NCORES)], 0)[None]
    cv_s = np.concatenate([np.asarray(R[i]["cv_s"]).reshape(NS, 2, 1024) for i in range(NCORES)], 0)[None]
    sv_s = np.concatenate([np.asarray(R[i]["sv_s"]).reshape(NS, 1, 8, 128) for i in range(NCORES)], 0)[None]
    outs = (y_prompt, y_sample, ak_p, av_p, ik_p, cv_p, mk_p, mv_p, ak_s, av_s, ik_s, cv_s, sv_s)
    return tuple(np.ascontiguousarray(o, dtype=np.float32) for o in outs)
```
